# Optimizing a Trainium2 kernel written in Bass

```python
import jax, jax.numpy as jnp
from jax import lax
import numpy as np

D_MODEL = 1024
BATCH = 8
SEQ = 4096
DEPTH = 1

N_META = 16
CHUNK = 64
N_PAD = CHUNK - N_META
GLA_HEADS = 4
GLA_DK = 128
GLA_DV = 256
GLA_KEY = GLA_HEADS * GLA_DK
GLA_VAL = GLA_HEADS * GLA_DV
GATE_RANK = 16
GATE_TAU = 16.0
SSM_WIDTH = D_MODEL
SSM_GROUP = 16
SSM_GROUPS = SSM_WIDTH // SSM_GROUP
SSM_STATE = 64
D_FF = 4 * D_MODEL
EPS = 1e-6

Q_END = GLA_KEY
K_END = Q_END + GLA_KEY
V_END = K_END + GLA_VAL
R_END = V_END + GLA_VAL
A_END = R_END + GATE_RANK
U_END = A_END + SSM_WIDTH
G1_END = U_END + D_MODEL
IN_WIDTH = G1_END + D_MODEL

kernel_name = "hybrid_gla_s5_gated_block"


def rmsnorm(x, g):
    xf = x.astype(jnp.float32)
    ms = jnp.mean(xf * xf, axis=-1, keepdims=True)
    return (xf * lax.rsqrt(ms + EPS) * g.astype(jnp.float32)).astype(x.dtype)


def gla_chunked(q, k, v, log_a):
    bsz, p, h, dk = q.shape
    dv = v.shape[-1]
    nc = p // CHUNK

    def to_chunks(t):
        return t.reshape(bsz, nc, CHUNK, h, t.shape[-1]).transpose(1, 0, 3, 2, 4)

    qc, kc, vc = to_chunks(q), to_chunks(k), to_chunks(v)
    b = jnp.cumsum(to_chunks(log_a).astype(jnp.float32), axis=3)
    b_ref = b[:, :, :, CHUNK // 2:CHUNK // 2 + 1, :]
    q_in = qc * jnp.exp(b - b_ref)
    k_in = kc * jnp.exp(b_ref - b)
    mask = jnp.tril(jnp.ones((CHUNK, CHUNK), dtype=bool))
    scores = jnp.where(mask, jnp.einsum('nbhid,nbhjd->nbhij', q_in, k_in), 0.0)
    o_intra = jnp.einsum('nbhij,nbhjv->nbhiv', scores, vc.astype(jnp.float32))

    b_last = b[:, :, :, -1:, :]
    q_inter = qc * jnp.exp(b)
    k_state = kc * jnp.exp(b_last - b)
    decay_chunk = jnp.exp(b_last[:, :, :, 0, :])

    def step(state, inp):
        qi, ks, vv, dc = inp
        o = jnp.einsum('bhid,bhdv->bhiv', qi, state)
        new_state = dc[..., None] * state + jnp.einsum('bhjd,bhjv->bhdv', ks, vv.astype(jnp.float32))
        return new_state, o

    s0 = jnp.zeros((bsz, h, dk, dv), jnp.float32)
    _, o_inter = lax.scan(step, s0, (q_inter, k_state, vc, decay_chunk))
    o = (o_intra + o_inter).transpose(1, 0, 3, 2, 4).reshape(bsz, p, h, dv)
    return o.astype(v.dtype)


def _complex_combine(e1, e2):
    a1r, a1i, b1r, b1i = e1
    a2r, a2i, b2r, b2i = e2
    return (a2r * a1r - a2i * a1i,
            a2r * a1i + a2i * a1r,
            a2r * b1r - a2i * b1i + b2r,
            a2r * b1i + a2i * b1r + b2i)


def s5_ssm(u, a_re, a_im, log_step, b_re, b_im, c_re, c_im, d_skip):
    bsz, p, _ = u.shape
    g, n, hg = SSM_GROUPS, SSM_STATE, SSM_GROUP
    ar = a_re.astype(jnp.float32)
    ai = a_im.astype(jnp.float32)
    dt = jnp.exp(log_step.astype(jnp.float32))[:, None]
    mag = jnp.exp(ar * dt)
    lam_re, lam_im = mag * jnp.cos(ai * dt), mag * jnp.sin(ai * dt)
    zr, zi = lam_re - 1.0, lam_im
    den = ar * ar + ai * ai
    fr = (zr * ar + zi * ai) / den
    fi = (zi * ar - zr * ai) / den
    br, bi = b_re.astype(jnp.float32), b_im.astype(jnp.float32)
    bb_re = fr[..., None] * br - fi[..., None] * bi
    bb_im = fr[..., None] * bi + fi[..., None] * br
    cr, ci = c_re.astype(jnp.float32), c_im.astype(jnp.float32)

    nc = p // CHUNK
    uc = u.reshape(bsz, nc, CHUNK, g, hg).transpose(1, 0, 2, 3, 4)
    lam_b_re = jnp.broadcast_to(lam_re, (bsz, CHUNK, g, n))
    lam_b_im = jnp.broadcast_to(lam_im, (bsz, CHUNK, g, n))

    def step(carry, u_blk):
        h_re, h_im = carry
        uf = u_blk.astype(jnp.float32)
        bu_re = jnp.einsum('bcgh,gnh->bcgn', uf, bb_re)
        bu_im = jnp.einsum('bcgh,gnh->bcgn', uf, bb_im)
        acc_re, acc_im, x_re, x_im = lax.associative_scan(
            _complex_combine, (lam_b_re, lam_b_im, bu_re, bu_im), axis=1)
        s_re = x_re + acc_re * h_re[:, None] - acc_im * h_im[:, None]
        s_im = x_im + acc_re * h_im[:, None] + acc_im * h_re[:, None]
        y = jnp.einsum('bcgn,ghn->bcgh', s_re, cr) - jnp.einsum('bcgn,ghn->bcgh', s_im, ci)
        return (s_re[:, -1], s_im[:, -1]), y

    carry0 = (jnp.zeros((bsz, g, n), jnp.float32), jnp.zeros((bsz, g, n), jnp.float32))
    _, y = lax.scan(step, carry0, uc)
    y = y.transpose(1, 0, 2, 3, 4).reshape(bsz, p, g * hg)
    return (y + d_skip.astype(jnp.float32) * u.astype(jnp.float32)).astype(u.dtype)


def setup_inputs(seed: int = 0) -> dict:
    key = jax.random.key(seed)
    ks = jax.random.split(key, 24)
    nrm = lambda k, shape, s: jax.random.normal(k, shape, jnp.float32) * s
    gain = lambda k, shape: 1.0 + 0.05 * jax.random.normal(k, shape, jnp.float32)
    L = DEPTH
    x = jax.random.normal(ks[0], (BATCH, SEQ, D_MODEL), jnp.float32)
    meta_tokens = nrm(ks[1], (N_META, D_MODEL), 1.0)
    g_mix_pre = gain(ks[2], (L, D_MODEL))
    w_in = nrm(ks[3], (L, D_MODEL, IN_WIDTH), D_MODEL ** -0.5)
    w_gate_up = nrm(ks[4], (L, GATE_RANK, GLA_KEY), GATE_RANK ** -0.5)
    b_gate = 1.0 + 0.5 * jax.random.normal(ks[5], (L, GLA_KEY), jnp.float32)
    gla_norm_g = gain(ks[6], (L, GLA_HEADS, GLA_DV))
    w_o_gla = nrm(ks[7], (L, GLA_VAL, D_MODEL), GLA_VAL ** -0.5)
    n_idx = jnp.arange(SSM_STATE, dtype=jnp.float32)
    a_re = -0.5 + 0.01 * jax.random.normal(ks[8], (L, SSM_GROUPS, SSM_STATE), jnp.float32)
    a_im = jnp.broadcast_to(jnp.pi * n_idx, (L, SSM_GROUPS, SSM_STATE)) + 0.0
    log_step = jax.random.uniform(ks[9], (L, SSM_GROUPS), jnp.float32,
                                  minval=float(np.log(1e-3)), maxval=float(np.log(1e-1)))
    b_re = nrm(ks[10], (L, SSM_GROUPS, SSM_STATE, SSM_GROUP), (2 * SSM_GROUP) ** -0.5)
    b_im = nrm(ks[11], (L, SSM_GROUPS, SSM_STATE, SSM_GROUP), (2 * SSM_GROUP) ** -0.5)
    c_re = nrm(ks[12], (L, SSM_GROUPS, SSM_GROUP, SSM_STATE), (2 * SSM_STATE) ** -0.5)
    c_im = nrm(ks[13], (L, SSM_GROUPS, SSM_GROUP, SSM_STATE), (2 * SSM_STATE) ** -0.5)
    d_skip = nrm(ks[14], (L, SSM_WIDTH), 1.0)
    w_glu = nrm(ks[15], (L, SSM_WIDTH, 2 * D_MODEL), SSM_WIDTH ** -0.5)
    b_glu = nrm(ks[16], (L, 2 * D_MODEL), 0.02)
    w_out = nrm(ks[17], (L, D_MODEL, D_MODEL), D_MODEL ** -0.5)
    g_mix_post = gain(ks[18], (L, D_MODEL))
    g_ffn_pre = gain(ks[19], (L, D_MODEL))
    w_ff1 = nrm(ks[20], (L, D_MODEL, D_FF), D_MODEL ** -0.5)
    w_ff2 = nrm(ks[21], (L, D_FF, D_MODEL), D_FF ** -0.5)
    g_ffn_post = gain(ks[22], (L, D_MODEL))
    return {"x": x, "meta_tokens": meta_tokens, "g_mix_pre": g_mix_pre, "w_in": w_in,
            "w_gate_up": w_gate_up, "b_gate": b_gate, "gla_norm_g": gla_norm_g, "w_o_gla": w_o_gla,
            "a_re": a_re, "a_im": a_im, "log_step": log_step, "b_re": b_re, "b_im": b_im,
            "c_re": c_re, "c_im": c_im, "d_skip": d_skip, "w_glu": w_glu, "b_glu": b_glu,
            "w_out": w_out, "g_mix_post": g_mix_post, "g_ffn_pre": g_ffn_pre, "w_ff1": w_ff1,
            "w_ff2": w_ff2, "g_ffn_post": g_ffn_post}


def reference(x, meta_tokens, g_mix_pre, w_in, w_gate_up, b_gate, gla_norm_g, w_o_gla,
              a_re, a_im, log_step, b_re, b_im, c_re, c_im, d_skip, w_glu, b_glu,
              w_out, g_mix_post, g_ffn_pre, w_ff1, w_ff2, g_ffn_post):
    bsz, seq, dm = x.shape
    meta = jnp.broadcast_to(meta_tokens.astype(x.dtype)[None], (bsz, N_META, dm))
    h = jnp.concatenate([meta, x], axis=1)
    p = N_PAD + N_META + seq

    def pad(t):
        return jnp.pad(t, ((0, 0), (N_PAD, 0), (0, 0)))

    def heads(t, d):
        return t.reshape(bsz, p, GLA_HEADS, d)

    for l in range(DEPTH):
        xn = rmsnorm(h, g_mix_pre[l])
        proj = xn @ w_in[l]
        q = proj[..., :Q_END]
        k = proj[..., Q_END:K_END]
        v = proj[..., K_END:V_END]
        r = proj[..., V_END:R_END]
        a_low = proj[..., R_END:A_END]
        u = proj[..., A_END:U_END]
        z_gla = proj[..., U_END:G1_END]
        z_ssm = proj[..., G1_END:]

        log_a = jax.nn.log_sigmoid((a_low @ w_gate_up[l] + b_gate[l]).astype(jnp.float32)) / GATE_TAU
        o = gla_chunked(heads(pad(q), GLA_DK) * (GLA_DK ** -0.5), heads(pad(k), GLA_DK),
                        heads(pad(v), GLA_DV), heads(pad(log_a), GLA_DK))[:, N_PAD:]
        o = rmsnorm(o, gla_norm_g[l]).reshape(bsz, N_META + seq, GLA_VAL) * jax.nn.silu(r)
        y_gla = o @ w_o_gla[l]

        y_s = s5_ssm(pad(u), a_re[l], a_im[l], log_step[l], b_re[l], b_im[l],
                     c_re[l], c_im[l], d_skip[l])[:, N_PAD:]
        g_s = jax.nn.gelu(y_s)
        glu = g_s @ w_glu[l] + b_glu[l]
        y_ssm = glu[..., :D_MODEL] * jax.nn.sigmoid(glu[..., D_MODEL:])

        mixed = jax.nn.sigmoid(z_gla) * y_gla + jax.nn.sigmoid(z_ssm) * y_ssm
        h = h + rmsnorm(mixed @ w_out[l], g_mix_post[l])

        hn = rmsnorm(h, g_ffn_pre[l])
        f = jnp.square(jax.nn.relu(hn @ w_ff1[l])) @ w_ff2[l]
        h = h + rmsnorm(f, g_ffn_post[l])

    return h[:, N_META:]
```

```python
import contextlib
import numpy as np
import concourse.bass as bass
import concourse.mybir as mybir
from concourse.bass_utils import run_bass_kernel_spmd

F32 = mybir.dt.float32
BF16 = mybir.dt.bfloat16
AF = mybir.ActivationFunctionType
ALU = mybir.AluOpType

SELF_SYNC = True


class Reg:
    __slots__ = ("w", "r")

    def __init__(self):
        self.w = {}
        self.r = {}


class TT:
    def __init__(self, h, name, sub=None, pstep=None):
        self.h = h
        self.name = name
        self.sub = sub
        self.pstep = pstep
        self.regs = {}

    def __getitem__(self, idx):
        return self.h[idx]

    def regions(self, ap):
        if self.sub is None:
            ks = (0,)
        else:
            off = ap.offset
            dims = list(ap.ap)
            if self.pstep:
                off = off % self.pstep
                dims = dims[1:]
            lo = hi = off
            for st, n in dims:
                if st >= 0:
                    hi += st * (n - 1)
                else:
                    lo += st * (n - 1)
            ks = range(lo // self.sub, hi // self.sub + 1)
        out = []
        for k in ks:
            r = self.regs.get(k)
            if r is None:
                r = self.regs[k] = Reg()
            out.append(r)
        return out


class Op:
    __slots__ = ("eng", "fn", "waits", "flag", "idx", "dma", "semval")

    def __init__(self, eng, fn, dma):
        self.eng = eng
        self.fn = fn
        self.waits = {}
        self.flag = False
        self.dma = dma
        self.semval = None


class Sched:
    ENGS = ("pe", "act", "dve", "pool", "sp")

    def __init__(self, nc):
        self.nc = nc
        self.ops = {e: [] for e in self.ENGS}
        self.dma_ops = []
        self.seen = {e: {} for e in self.ENGS}

    def _dep(self, waits, chan, idx):
        if idx is None:
            return
        if waits.get(chan, -1) < idx:
            waits[chan] = idx

    def op(self, eng, fn, reads=(), writes=(), dma=False):
        o = Op(eng, fn, dma)
        waits = {}
        for t in reads:
            for c, i in t.w.items():
                self._dep(waits, c, i)
        for t in writes:
            for c, i in t.w.items():
                self._dep(waits, c, i)
            for c, i in t.r.items():
                self._dep(waits, c, i)
        seen = self.seen[eng]
        for c, i in waits.items():
            if c == eng and not dma:
                if eng == "pe" or not SELF_SYNC:
                    continue
            if seen.get(c, -1) >= i:
                continue
            seen[c] = i
            o.waits[c] = i
            self._chan_ops(c)[i].flag = True
        if dma:
            chan = ("d", len(self.dma_ops))
            self.dma_ops.append(o)
            o.idx = 0
            o.flag = True
            self.ops[eng].append(o)
        else:
            chan = eng
            o.idx = len(self.ops[eng])
            self.ops[eng].append(o)
        for t in reads:
            t.r[chan] = o.idx
        for t in writes:
            t.w[chan] = o.idx
        return o

    def _chan_ops(self, c):
        if isinstance(c, tuple):
            return [self.dma_ops[c[1]]]
        return self.ops[c]

    def check(self):
        ptr = {e: 0 for e in self.ENGS}
        done_dma = set()
        progress = True
        while progress:
            progress = False
            for e in self.ENGS:
                while ptr[e] < len(self.ops[e]):
                    o = self.ops[e][ptr[e]]
                    ok = True
                    for c, i in o.waits.items():
                        if isinstance(c, tuple):
                            if c not in done_dma:
                                ok = False
                        elif ptr[c] <= i:
                            ok = False
                    if not ok:
                        break
                    if o.dma:
                        done_dma.add(("d", self.dma_ops.index(o)))
                    ptr[e] += 1
                    progress = True
        stuck = {e: (ptr[e], len(self.ops[e])) for e in self.ENGS if ptr[e] < len(self.ops[e])}
        return stuck

    def emit(self, final_waits):
        nc = self.nc
        stuck = self.check()
        assert not stuck, ("DEADLOCK", stuck)
        fin = Op("sp", None, False)
        for t in final_waits:
            for c, i in list(t.w.items()):
                self._dep(fin.waits, c, i)
                self._chan_ops(c)[i].flag = True
        with contextlib.ExitStack() as es:
            esem = {e: es.enter_context(nc.semaphore("s_" + e)) for e in self.ENGS}
            for e in self.ENGS:
                cnt = 0
                for o in self.ops[e]:
                    if o.dma:
                        continue
                    if o.flag:
                        cnt += 1
                        o.semval = cnt
            dsem = {}
            dcnt = {}
            for o in self.dma_ops:
                k = o.fn.semkey
                if k not in dsem:
                    dsem[k] = es.enter_context(nc.semaphore("sd_%s" % (k,)))
                    dcnt[k] = 0
                dcnt[k] += 16
                o.semval = (dsem[k], dcnt[k])

            def wait_list(o):
                res = []
                for c, i in o.waits.items():
                    if isinstance(c, tuple):
                        s, v = self.dma_ops[c[1]].semval
                    else:
                        s, v = esem[c], self.ops[c][i].semval
                    res.append((s, v))
                return res

            block = es.enter_context(nc.Block())

            def run(ename, engobj):
                for o in self.ops[ename]:
                    for s, v in wait_list(o):
                        engobj.wait_ge(s, v)
                    ins = o.fn(engobj)
                    if o.dma:
                        ins.then_inc(o.semval[0], 16)
                    elif o.flag:
                        ins.then_inc(esem[ename], 1)
                if ename == "sp":
                    for s, v in wait_list(fin):
                        engobj.wait_ge(s, v)

            @block.tensor
            def _(e):
                run("pe", e)

            @block.scalar
            def _(e):
                run("act", e)

            @block.vector
            def _(e):
                run("dve", e)

            @block.gpsimd
            def _(e):
                run("pool", e)

            @block.sync
            def _(e):
                run("sp", e)


class DmaFn:
    def __init__(self, semkey, f):
        self.semkey = semkey
        self.f = f

    def __call__(self, e):
        return self.f(e)


class KB:
    def __init__(self, nc):
        self.nc = nc
        self.S = Sched(nc)
        self.es = contextlib.ExitStack()
        self.reg = {}
        self.psum_names = set()

    def _add(self, h, name, sub, pstep):
        t = TT(h, name, sub, pstep)
        self.reg[name] = t
        return t

    def sb(self, name, shape, dt, sub=None):
        h = self.es.enter_context(self.nc.sbuf_tensor(name, list(shape), dt))
        fs = int(np.prod(shape[1:]))
        return self._add(h, name, sub, fs)

    def ps(self, name, shape, dt, sub=None):
        h = self.es.enter_context(self.nc.psum_tensor(name, list(shape), dt))
        self.psum_names.add(name)
        fs = int(np.prod(shape[1:]))
        return self._add(h, name, sub, fs)

    def dram(self, name, shape, dt, kind, sub=None):
        h = self.nc.dram_tensor(name, list(shape), dt, kind=kind)
        return self._add(h.ap(), name, sub, None)

    def regs(self, aps):
        out = []
        for a in aps:
            if a is None or isinstance(a, (int, float)):
                continue
            out.extend(self.reg[a.tensor.name].regions(a))
        return out

    def op(self, eng, fn, outs, ins, dma=False):
        ps_ins = [a for a in ins if a is not None and not isinstance(a, (int, float))
                  and a.tensor.name in self.psum_names]
        return self.S.op(eng, fn, self.regs(ins), self.regs(list(outs) + ps_ins), dma=dma)

    def mm(self, out, lhsT, rhs, start=True, stop=True, **kw):
        return self.op("pe", lambda e: e.matmul(out, lhsT, rhs, start=start, stop=stop, **kw),
                       [out], [lhsT, rhs])

    def tr(self, out, in_, ident):
        return self.op("pe", lambda e: e.transpose(out, in_, ident), [out], [in_, ident])

    def act(self, out, in_, func, bias=0.0, scale=1.0, accum_out=None, eng="act"):
        outs = [out] + ([accum_out] if accum_out is not None else [])
        kw = {}
        if accum_out is not None:
            kw["accum_out"] = accum_out
        return self.op(eng, lambda e: e.activation(out, in_, func, bias=bias, scale=scale, **kw),
                       outs, [in_, bias, scale])

    def tt(self, eng, out, in0, in1, op):
        return self.op(eng, lambda e: e.tensor_tensor(out, in0, in1, op), [out], [in0, in1])

    def ts(self, eng, out, in0, s1, s2, op0, op1=None):
        if op1 is None:
            return self.op(eng, lambda e: e.tensor_scalar(out, in0, s1, None, op0), [out], [in0, s1])
        return self.op(eng, lambda e: e.tensor_scalar(out, in0, s1, s2, op0, op1), [out], [in0, s1, s2])

    def stt(self, eng, out, in0, scalar, in1, op0, op1):
        return self.op(eng, lambda e: e.scalar_tensor_tensor(out, in0, scalar, in1, op0, op1),
                       [out], [in0, scalar, in1])

    def copy(self, eng, out, in_):
        if eng == "act":
            return self.op(eng, lambda e: e.copy(out, in_), [out], [in_])
        return self.op(eng, lambda e: e.tensor_copy(out, in_), [out], [in_])

    def memset(self, eng, ap, val):
        return self.op(eng, lambda e: e.memset(ap, val), [ap], [])

    def dma(self, eng, out, in_, semkey, **kw):
        return self.op(eng, DmaFn(semkey, lambda e: e.dma_start(out, in_, **kw)), [out], [in_], dma=True)

    def finish(self, out_tt):
        regs = []
        for r in out_tt.regs.values():
            regs.append(r)
        self.S.emit(regs)
        self.es.close()


P = 128
D = 1024
KT = 8
TB = 512
S5T = 8
NSLOT = 4
EPS = 1e-6
PI = float(np.pi)

C_Q, C_K, C_V, C_R, C_A, C_U, C_ZG, C_ZS = 0, 512, 1024, 2048, 3072, 3088, 4112, 5136

ORDER_META = ["u0", "u1", "rb0", "rb1", "rb2", "rb3", "k", "v0", "v1"]
ORDER_X = (["u0", "u1", "rb0", "rb1", "rb2", "rb3", "k", "q", "v0", "v1",
            "r0", "r1", "zg0", "zg1",
            "wo0", "wo1",
            "kl0", "cp0", "cp1", "kl1", "cp2", "cp3",
            "glu0", "glu1", "glu2", "glu3", "zs0", "zs1",
            "wout0", "wout1"]
           + ["ff1_%d" % i for i in range(8)] + ["ff2_%d" % i for i in range(8)])


def build_program(nblk, taps=None, stop_after=None):
    nc = bass.Bass("TRN2", target_bir_lowering=False)
    k = KB(nc)
    SEQ = nblk * TB

    x_d = k.dram("x", [SEQ, D], F32, "ExternalInput")
    meta_d = k.dram("meta", [16, D], F32, "ExternalInput")
    pvec_d = k.dram("pvec", [P, 64], F32, "ExternalInput")
    w_in_d = k.dram("w_in", [D, 6160], F32, "ExternalInput")
    w_o_d = k.dram("w_o", [D, D], F32, "ExternalInput")
    w_glu_d = k.dram("w_glu", [D, 2 * D], F32, "ExternalInput")
    w_out_d = k.dram("w_out", [D, D], F32, "ExternalInput")
    w_ff1_d = k.dram("w_ff1", [D, 4 * D], F32, "ExternalInput")
    w_ff2_d = k.dram("w_ff2", [4 * D, D], F32, "ExternalInput")
    wgu_d = k.dram("wgu", [16, 512], F32, "ExternalInput")
    bgate_d = k.dram("bgate", [1, 512], F32, "ExternalInput")
    are_d = k.dram("are", [64, 64], F32, "ExternalInput")
    aim_d = k.dram("aim", [64, 64], F32, "ExternalInput")
    lstep_d = k.dram("lstep", [64, 1], F32, "ExternalInput")
    bre_d = k.dram("bre", [64, 1024], F32, "ExternalInput")
    bim_d = k.dram("bim", [64, 1024], F32, "ExternalInput")
    cre_d = k.dram("cre", [64, 1024], F32, "ExternalInput")
    cim_d = k.dram("cim", [64, 1024], F32, "ExternalInput")
    y_d = k.dram("y", [SEQ, D], F32, "ExternalOutput")

    chunks = {}

    def wchunk(name, src, r0, c0):
        t = k.dram("sc_" + name, [1024, 512], BF16, "Internal")
        chunks[name] = (t, src, r0, c0)

    wchunk("q", w_in_d, 0, C_Q)
    wchunk("k", w_in_d, 0, C_K)
    for i in range(2):
        wchunk("v%d" % i, w_in_d, 0, C_V + 512 * i)
        wchunk("r%d" % i, w_in_d, 0, C_R + 512 * i)
        wchunk("u%d" % i, w_in_d, 0, C_U + 512 * i)
        wchunk("zg%d" % i, w_in_d, 0, C_ZG + 512 * i)
        wchunk("zs%d" % i, w_in_d, 0, C_ZS + 512 * i)
        wchunk("wo%d" % i, w_o_d, 0, 512 * i)
        wchunk("wout%d" % i, w_out_d, 0, 512 * i)
    for i in range(4):
        wchunk("glu%d" % i, w_glu_d, 0, 512 * i)
    for i in range(8):
        wchunk("ff1_%d" % i, w_ff1_d, 0, 512 * i)
    for i in range(8):
        wchunk("ff2_%d" % i, w_ff2_d, 1024 * (i % 4), 512 * (i // 4))
    s5chunks = {}
    for nm in ["rb0", "rb1", "rb2", "rb3", "kl0", "kl1", "cp0", "cp1", "cp2", "cp3"]:
        s5chunks[nm] = k.dram("sc_" + nm, [P, 4096], BF16, "Internal")
    walow_sc = k.dram("sc_alow", [1024, 16], BF16, "Internal")
    wgu_sc = k.dram("sc_wgu", [16, 512], BF16, "Internal")
    Gall_d = k.dram("s5_G", [64, 2, 64, 9, 16], F32, "Internal")
    Hd_d = k.dram("s5_H", [2, 64, 16, 8, 64], F32, "Internal")
    Bd_d = k.dram("s5_B", [64, 2, 64, 16], F32, "Internal")

    AF_ = k.sb("AF32", [P, 12288], F32, sub=512)
    hT = AF_[:, 0:4096].rearrange("p (a b) -> p a b", b=TB)
    FA = AF_[:, 4096:8192].rearrange("p (a b) -> p a b", b=TB)
    FB = AF_[:, 8192:12288].rearrange("p (a b) -> p a b", b=TB)
    LA = k.sb("LA", [P, 2048], F32, sub=128)
    BB = k.sb("BB", [P, 4, 4096], BF16, sub=512)
    Bv = [BB[:, i, :].rearrange("p (a b) -> p a b", b=TB) for i in range(4)]
    AR = k.sb("ARENA", [P, 16384], BF16, sub=512)
    qT = AR[:, 0:2048].rearrange("p (a b) -> p a b", b=TB)
    kT = AR[:, 2048:4096].rearrange("p (a b) -> p a b", b=TB)
    ktok = AR[:, 4096:6144].rearrange("p (a b) -> p a b", b=512)
    vtok = AR[:, 6144:10240].rearrange("p (a b) -> p a b", b=1024)
    Xb = AR[:, 10240:14336].rearrange("p (r g c) -> p r g c", r=2, g=32)
    hid = AR[:, 0:16384].rearrange("p (a b) -> p a b", b=TB)
    sq = AR[:, 12288:16384].rearrange("p (a b) -> p a b", b=TB)
    RING = k.sb("RING", [P, NSLOT, 4096], BF16, sub=4096)
    XIN = k.sb("XIN", [P, 3, 1024], F32, sub=1024)
    XOUT = k.sb("XOUT", [P, 2, 1024], F32, sub=1024)
    SST = k.sb("SST", [P, 4, 256], F32, sub=256)
    SSB = k.sb("SSB", [P, 4, 256], BF16, sub=256)
    GT = k.sb("GT", [P, 1040], F32, sub=512)
    GB = k.sb("GB", [P, 1536], BF16, sub=512)
    KTK = k.sb("KTK", [P, 2, 512], BF16, sub=512)
    ATK = k.sb("ATK", [P, 512], F32)
    RS = k.sb("RSTD", [P, 2, 512], F32, sub=512)
    LNT = k.sb("LNT", [P, 512], F32)
    CONST = k.sb("CONST", [P, 1280], F32, sub=64)
    identF = CONST[:, 0:128]
    TRI = CONST[:, 128:258]
    colR = CONST[:, 258:259]
    pv = CONST[:, 320:384]
    mh = CONST[:, 384:386]
    mga = CONST[:, 386:388]
    bdm = CONST[:, 512:640]
    ARR = CONST[:, 640:704]
    AIS = CONST[:, 704:768]
    XP = CONST[:, 768:832]
    ones1 = CONST[0:1, 896:1024]
    RT1 = CONST[:, 1024:1088]
    RT2 = CONST[:, 1088:1152]
    CB = k.sb("CONSTB", [P, 1024], BF16, sub=128)
    identB = CB[:, 0:128]
    onesD = CB[:, 128:256]
    ones256 = CB[:, 256:384]
    maskC = CB[:, 384:512]
    al17 = CB[0:32, 512:1024]
    WAL = k.sb("WAL", [P, 8, 16], BF16)
    WGU = k.sb("WGU", [16, 512], BF16)
    BGT = k.sb("BGT", [1, 512], F32)
    S5S = LA
    PS = [k.ps("ps%d" % i, [P, 512], F32) for i in range(8)]
    bank_ctr = [0]
    held = set()

    def bank(hold=False):
        for _ in range(8):
            i = bank_ctr[0] % 8
            bank_ctr[0] += 1
            if i not in held:
                if hold:
                    held.add(i)
                return PS[i]
        raise RuntimeError("all PSUM banks held")

    def release(b):
        held.discard(PS.index(b))

    def early_exit():
        regs = []
        for t in k.reg.values():
            if t.pstep is None:
                regs.extend(t.regs.values())
        k.S.emit(regs)
        k.es.close()
        return nc, []

    gpre, gpost, gfpre, gfpost, ggla = (pv[:, 0:8], pv[:, 8:16], pv[:, 16:24], pv[:, 24:32], pv[:, 32:40])
    bglu = pv[:, 40:56]
    dsk = pv[:, 56:64]

    k.dma("sp", pv, pvec_d[:, :], "c0")
    k.dma("sp", BGT[:, :], bgate_d[:, :], "c1")
    k.memset("pool", identF, 1.0)
    k.op("pool", lambda e: e.affine_select(out=identF, in_=identF, compare_op=ALU.is_equal, fill=0.0, base=0,
                                           pattern=[[-1, 128]], channel_multiplier=1), [identF], [identF])
    k.copy("pool", identB, identF)
    k.memset("pool", onesD, 1.0 / 1024.0)
    k.memset("pool", ones256, 1.0 / 256.0)
    k.memset("pool", CONST[0:1, 896:1024], 1.0)
    k.memset("pool", CB[0:32, 512:1024], 1.0)
    k.memset("pool", XP, 0.0)
    k.memset("pool", SST[:, :, :], 0.0)
    k.memset("pool", SSB[:, :, :], 0.0)
    U = GT[:, 0:128]
    k.memset("pool", U, 1.0)
    k.op("pool", lambda e: e.affine_select(out=U, in_=U, compare_op=ALU.is_ge, fill=0.0, base=0,
                                           pattern=[[1, 128]], channel_multiplier=-1), [U], [U])
    k.copy("pool", maskC, U)
    k.memset("pool", colR, 1.0)
    k.op("pool", lambda e: e.affine_select(out=colR, in_=colR, compare_op=ALU.is_ge, fill=0.0, base=64,
                                           pattern=[[0, 1]], channel_multiplier=-1), [colR], [colR])
    k.ts("dve", TRI[:, 0:128], U, colR, -1.0 / 16.0, ALU.subtract, ALU.mult)
    k.ts("dve", TRI[:, 128:129], colR, -1.0 / 16.0, None, ALU.mult)
    k.ts("dve", TRI[:, 129:130], colR, 1.0 / 16.0, -1.0 / 16.0, ALU.mult, ALU.add)
    k.memset("pool", mh, 1.0)
    k.op("pool", lambda e: e.affine_select(out=mh, in_=mh, compare_op=ALU.is_ge, fill=0.0, base=0,
                                           pattern=[[-64, 2]], channel_multiplier=1), [mh], [mh])
    k.op("pool", lambda e: e.affine_select(out=mh, in_=mh, compare_op=ALU.is_ge, fill=0.0, base=63,
                                           pattern=[[64, 2]], channel_multiplier=-1), [mh], [mh])
    bd3 = bdm.rearrange("p (a b) -> p a b", b=16)
    k.memset("pool", bdm, 1.0)
    k.op("pool", lambda e: e.affine_select(out=bd3, in_=bd3, compare_op=ALU.is_ge, fill=0.0, base=0,
                                           pattern=[[-16, 8], [0, 16]], channel_multiplier=1), [bdm], [bdm])
    k.op("pool", lambda e: e.affine_select(out=bd3, in_=bd3, compare_op=ALU.is_ge, fill=0.0, base=15,
                                           pattern=[[16, 8], [0, 16]], channel_multiplier=-1), [bdm], [bdm])
    k.tt("pool", mga[:, 0:1], bdm[:, 0:1], bdm[:, 32:33], ALU.add)
    k.tt("pool", mga[:, 0:1], mga[:, 0:1], bdm[:, 64:65], ALU.add)
    k.tt("pool", mga[:, 0:1], mga[:, 0:1], bdm[:, 96:97], ALU.add)
    k.ts("pool", mga[:, 1:2], mga[:, 0:1], -1.0, 1.0, ALU.mult, ALU.add)

    for ga_ in range(2):
        Sel_ = LA[0:64, 1792 + 32 * ga_:1824 + 32 * ga_]
        k.memset("pool", Sel_, 1.0)
        k.op("pool", lambda e, Sel_=Sel_, ga_=ga_: e.affine_select(out=Sel_, in_=Sel_, compare_op=ALU.is_equal, fill=0.0,
                                                                  base=-ga_, pattern=[[-2, 32]], channel_multiplier=1),
             [Sel_], [Sel_])
    k.dma("pool", walow_sc[:, :], w_in_d[:, C_A:C_A + 16], "cw_alow")
    k.dma("pool", wgu_sc[:, :], wgu_d[:, :], "cw_wgu")
    cast_order = ["u0", "u1", "k", "v0", "v1", "q", "r0", "r1", "zg0", "zg1", "wo0", "wo1", "glu0", "glu1", "glu2",
                  "glu3", "zs0", "zs1", "wout0", "wout1"] + \
                 ["ff1_%d" % i for i in range(8)] + ["ff2_%d" % i for i in range(8)]
    for ci_, nm in enumerate(cast_order):
        t, wsrc, r0, c0 = chunks[nm]
        ins = [wsrc[r0:r0 + 1024, c0:c0 + 512]]
        if ci_ >= 4:
            ins.append(chunks[cast_order[ci_ - 4]][0][:, :])
        k.op("pool", DmaFn("cw_" + nm, lambda e, t=t, wsrc=wsrc, r0=r0, c0=c0: e.dma_start(
            t[:, :], wsrc[r0:r0 + 1024, c0:c0 + 512])), [t[:, :]], ins, dma=True)

    if stop_after == "casts":
        regs = []
        for t in k.reg.values():
            if t.pstep is None:
                regs.extend(t.regs.values())
        k.S.emit(regs)
        k.es.close()
        return nc, []

    def s(a, b):
        return S5S[:, a:b]

    def recip(eng, out, in_):
        return k.op(eng, lambda e: e.reciprocal(out, in_), [out], [in_])

    are, aim, dt, dt16 = s(0, 64), s(64, 128), s(128, 129), s(129, 130)
    lr, li, t1, t2, fr, fi, t3 = s(192, 256), s(256, 320), s(320, 384), s(384, 448), s(448, 512), s(512, 576), s(576, 640)
    LPr, LPi = s(640, 1216), s(1216, 1792)
    Sel0, Sel1 = s(1792, 1824), s(1824, 1856)

    def lpr(kk):
        return S5S[:, 640 + 64 * kk:704 + 64 * kk]

    def lpi(kk):
        return S5S[:, 1216 + 64 * kk:1280 + 64 * kk]

    for hf_ in range(2):
        rows = slice(64 * hf_, 64 * hf_ + 64)
        k.dma("sp", are[rows], are_d[:, :], "c4")
        k.dma("sp", aim[rows], aim_d[:, :], "c5")
        k.dma("sp", dt[rows], lstep_d[:, :], "c6")
    k.act(dt, dt, AF.Exp)
    k.ts("dve", dt16, dt, 1.0 / 16.0, None, ALU.mult)
    k.act(t3, are, AF.Exp, scale=dt16)
    k.ts("dve", t1, aim, dt16, None, ALU.mult)
    k.act(li, t1, AF.Sin)
    k.act(lr, t1, AF.Sin, scale=-1.0, bias=PI / 2)
    k.tt("dve", lr, lr, t3, ALU.mult)
    k.tt("dve", li, li, t3, ALU.mult)
    for _ in range(4):
        k.tt("dve", t1, lr, lr, ALU.mult)
        k.tt("dve", t2, li, li, ALU.mult)
        k.tt("dve", t3, lr, li, ALU.mult)
        k.tt("dve", lr, t1, t2, ALU.subtract)
        k.ts("dve", li, t3, 2.0, None, ALU.mult)
    k.memset("dve", lpr(0), 1.0)
    k.memset("dve", lpi(0), 0.0)
    k.copy("dve", lpr(1), lr)
    k.copy("dve", lpi(1), li)
    for kk in range(2, 9):
        k.tt("dve", t1, lpr(kk - 1), lr, ALU.mult)
        k.tt("dve", t2, lpi(kk - 1), li, ALU.mult)
        k.tt("dve", lpr(kk), t1, t2, ALU.subtract)
        k.tt("dve", t1, lpr(kk - 1), li, ALU.mult)
        k.tt("dve", t2, lpi(kk - 1), lr, ALU.mult)
        k.tt("dve", lpi(kk), t1, t2, ALU.add)
    k.tt("dve", t1, are, are, ALU.mult)
    k.tt("dve", t2, aim, aim, ALU.mult)
    k.tt("dve", t1, t1, t2, ALU.add)
    recip("dve", t3, t1)
    k.ts("dve", t1, lr, -1.0, None, ALU.add)
    k.tt("dve", fr, t1, are, ALU.mult)
    k.tt("dve", t2, li, aim, ALU.mult)
    k.tt("dve", fr, fr, t2, ALU.add)
    k.tt("dve", fr, fr, t3, ALU.mult)
    k.tt("dve", fi, li, are, ALU.mult)
    k.tt("dve", t2, t1, aim, ALU.mult)
    k.tt("dve", fi, fi, t2, ALU.subtract)
    k.tt("dve", fi, fi, t3, ALU.mult)

    if stop_after == "lam":
        return early_exit()
    bA = bank()
    dup = s(320, 448)
    for ri, lp in ((0, lpr(8)), (1, lpi(8))):
        k.copy("dve", dup[0:64, 0:64], lp[0:64])
        k.copy("dve", dup[0:64, 64:128], lp[0:64])
        for ga, Sel in ((0, Sel0), (1, Sel1)):
            k.mm(bA[:, 64 * ga + 32 * ri:64 * ga + 32 * ri + 32], dup[0:64], Sel[0:64])
    for ga in (0, 1):
        rows = slice(64 * ga, 64 * ga + 64)
        k.copy("dve", ARR[rows, 0:32], bA[rows, 64 * ga:64 * ga + 32])
        k.copy("dve", ARR[rows, 32:64], bA[rows, 64 * ga:64 * ga + 32])
        k.ts("dve", AIS[rows, 0:32], bA[rows, 64 * ga + 32:64 * ga + 64], -1.0, None, ALU.mult)
        k.copy("dve", AIS[rows, 32:64], bA[rows, 64 * ga + 32:64 * ga + 64])

    if stop_after == "amat":
        return early_exit()
    bre, bim, tmpA = AF_[:, 0:1024], AF_[:, 1024:2048], AF_[:, 2048:3072]
    for hf_ in range(2):
        rows = slice(64 * hf_, 64 * hf_ + 64)
        k.dma("sp", bre[rows], bre_d[:, :], "c9")
        k.dma("sp", bim[rows], bim_d[:, :], "c10")
    s1, s2 = AF_[:, 3072:4096], AF_[:, 4096:5120]
    fr3 = fr.rearrange("p (n o) -> p n o", o=1).broadcast_to([P, 64, 16])
    fi3 = fi.rearrange("p (n o) -> p n o", o=1).broadcast_to([P, 64, 16])

    def v3(a):
        return a.rearrange("p (n h) -> p n h", h=16)

    k.tt("dve", v3(s1), v3(bre), fr3, ALU.mult)
    k.tt("dve", v3(tmpA), v3(bim), fi3, ALU.mult)
    k.tt("dve", s1, s1, tmpA, ALU.subtract)
    k.tt("dve", v3(s2), v3(bim), fr3, ALU.mult)
    k.tt("dve", v3(tmpA), v3(bre), fi3, ALU.mult)
    k.tt("dve", s2, s2, tmpA, ALU.add)
    k.copy("dve", bre[0:64], s1[0:64])
    k.copy("dve", bim[0:64], s2[0:64])
    k.copy("dve", bre[64:128], s2[64:128])
    k.copy("dve", bim[64:128], s1[64:128])
    k.dma("sp", Bd_d.h.rearrange("g r n h -> g (r n h)"), AF_[0:64, 0:2048], "s5b")
    k.ts("dve", S5S[64:128, 1216:1792], S5S[64:128, 1216:1792], -1.0, None, ALU.mult)
    stageH = AF_[:, 3072:11264].rearrange("p (h j n) -> p h j n", j=8, n=64)
    for j in range(8):
        kk = 7 - j
        ov = stageH[:, :, j, :].rearrange("p h n -> p n h")
        Lr = lpr(kk).rearrange("p (n o) -> p n o", o=1).broadcast_to([P, 64, 16])
        Li = lpi(kk).rearrange("p (n o) -> p n o", o=1).broadcast_to([P, 64, 16])
        k.tt("dve", v3(tmpA), v3(bim), Li, ALU.mult)
        k.tt("dve", ov, v3(bre), Lr, ALU.mult)
        k.tt("dve", ov, ov, v3(tmpA), ALU.subtract)
    for ri in range(2):
        k.dma("sp", Hd_d.h[ri].rearrange("g h j n -> g h (j n)"),
              AF_[64 * ri:64 * ri + 64, 3072:11264].rearrange("p (h x) -> p h x", x=512), "s5h%d" % ri)
    if stop_after == "H":
        return early_exit()

    k.copy("dve", S5S[64:128, 320:448], S5S[64:128, 640:768])
    for c0 in range(0, 576, 64):
        a_ = S5S[64:128, 640 + c0:704 + c0]
        b_ = S5S[64:128, 1216 + c0:1280 + c0]
        t_ = S5S[64:128, 320:384]
        k.copy("dve", t_, a_)
        k.copy("dve", a_, b_)
        k.copy("dve", b_, t_)
    cre, cim = AF_[:, 0:1024], AF_[:, 1024:2048]
    for hf_ in range(2):
        rows = slice(64 * hf_, 64 * hf_ + 64)
        k.dma("sp", cre[rows], cre_d[:, :], "c7")
        k.dma("sp", cim[rows], cim_d[:, :], "c8")
    stageG = AF_[:, 3072:12288].rearrange("p (n k h) -> p n k h", k=9, h=16)
    cre3 = cre.rearrange("p (h n) -> p h n", n=64)
    cim3 = cim.rearrange("p (h n) -> p h n", n=64)
    tmp3 = tmpA.rearrange("p (h n) -> p h n", n=64)
    for kk in range(9):
        ov = stageG[:, :, kk, :].rearrange("p n h -> p h n")
        Lr = lpr(kk).rearrange("p (o n) -> p o n", o=1).broadcast_to([P, 16, 64])
        Li = lpi(kk).rearrange("p (o n) -> p o n", o=1).broadcast_to([P, 16, 64])
        k.tt("dve", tmp3, cim3, Li, ALU.mult)
        k.tt("dve", ov, cre3, Lr, ALU.mult)
        k.tt("dve", ov, ov, tmp3, ALU.subtract)
    for ri in range(2):
        k.dma("sp", Gall_d.h[:, ri, :, :, :].rearrange("g n k h -> g n (k h)"),
              AF_[64 * ri:64 * ri + 64, 3072:12288].rearrange("p (n x) -> p n x", x=144), "s5g%d" % ri)
    if stop_after == "G":
        return early_exit()

    Rm = AF_[:, 0:1024]
    for gq in range(4):
        k.dma("sp", Rm.rearrange("p (g h) -> p g h", h=16)[:, 16 * gq:16 * gq + 16, :],
              Bd_d.h.rearrange("g r n h -> (r n) g h")[:, 16 * gq:16 * gq + 16, :], "s5rm")
    if stop_after == "KLa":
        return early_exit()
    for hf in range(2):
        Lm = AF_[:, 4096:8192].rearrange("p (g t h) -> p g t h", t=8, h=16)
        src = Gall_d.h[32 * hf:32 * hf + 32].rearrange("g r n k h -> (r n) g (k h)")[:, :, 0:128]
        k.dma("sp", AF_[:, 4096:8192].rearrange("p (g x) -> p g x", x=128), src, "s5lm")
        if stop_after == "KLb":
            return early_exit()
        KLst = Bv[0].rearrange("p a b -> p (a b)").rearrange("p (c t m) -> p c t m", t=8, m=128)
        for ctl in range(4):
            ct = 4 * hf + ctl
            for tg in range(2):
                b = bank()
                for tt_ in range(4):
                    tau = 4 * tg + tt_
                    k.mm(b[:, 128 * tt_:128 * tt_ + 128], Rm[:, 128 * ct:128 * ct + 128],
                         Lm[:, 8 * ctl:8 * ctl + 8, tau, :])
                for tt_ in range(4):
                    tau = 4 * tg + tt_
                    if tau == 0:
                        tf = GT[:, 0:128]
                        k.tt("dve", tf, b[:, 0:128], bdm, ALU.mult)
                        k.stt("dve", KLst[:, ctl, 0, :], identF, dsk[:, ct:ct + 1], tf, ALU.mult, ALU.add)
                    else:
                        k.tt("dve", KLst[:, ctl, tau, :], b[:, 128 * tt_:128 * tt_ + 128], bdm, ALU.mult)
        if stop_after in ("KLc", "KLc1", "KLc2", "KLc3"):
            return early_exit()
        k.dma("sp", s5chunks["kl%d" % hf][:, :], Bv[0].rearrange("p a b -> p (a b)"), "s5kl")

    if stop_after == "KL":
        return early_exit()
    for cp in range(4):
        RBst = Bv[1 + cp % 2].rearrange("p a b -> p (a b)").rearrange("p (c j r a n) -> p c j r a n", j=8, r=2, a=2, n=64)
        for ctl in range(2):
            ct = 2 * cp + ctl
            RBf = AF_[:, 1024 + 1024 * (ct % 2):2048 + 1024 * (ct % 2)]
            src = Hd_d.h[:, 8 * ct:8 * ct + 8].rearrange("r g h j n -> (g h) r (j n)")
            k.dma("sp", RBf.rearrange("p (r x) -> p r x", r=2), src, "s5rbf%d" % (ct % 2))
            RBf4 = RBf.rearrange("p (r j n) -> p r j n", r=2, n=64)
            for ri in range(2):
                in0 = RBf4[:, ri, :, :].rearrange("p j (o n) -> p j o n", o=1).broadcast_to([P, 8, 2, 64])
                in1 = mga.rearrange("p (o a q) -> p o a q", o=1, q=1).broadcast_to([P, 8, 2, 64])
                k.tt("dve", RBst[:, ctl, :, ri, :, :], in0, in1, ALU.mult)
        k.dma("sp", s5chunks["rb%d" % cp][:, :], Bv[1 + cp % 2].rearrange("p a b -> p (a b)"), "s5rb%d" % (cp % 2))

    if stop_after == "RB":
        return early_exit()
    CPfs = []
    for cq in range(4):
        CPf = AF_[:, 8192 + 2048 * (cq % 2):10240 + 2048 * (cq % 2)].rearrange("p (g r x) -> p g r x", r=2, x=128)
        CPfs.append(CPf)
    for cq in range(4):
        CPf = CPfs[cq]
        for ga in range(2):
            for ri in range(2):
                src = Gall_d.h.rearrange("(gp a) r n k h -> a n gp r (k h)", a=2)[ga][:, 8 * cq:8 * cq + 8, ri, 16:144]
                k.dma("sp", CPf[64 * ga:64 * ga + 64, :, ri, :], src, "s5cpf%d" % (cq % 2))
        CPst = Bv[3 if cq % 2 else 0].rearrange("p a b -> p (a b)").rearrange("p (g i r a h) -> p g i r a h", i=8, r=2, a=2, h=16)
        for gpl in range(8):
            for ri in range(2):
                in0 = CPf[:, gpl, ri, :].rearrange("p (i o h) -> p i o h", o=1, h=16).broadcast_to([P, 8, 2, 16])
                in1 = mh.rearrange("p (o a q) -> p o a q", o=1, q=1).broadcast_to([P, 8, 2, 16])
                k.tt("dve", CPst[:, gpl, :, ri, :, :], in0, in1, ALU.mult)
        k.dma("sp", s5chunks["cp%d" % cq][:, :], Bv[3 if cq % 2 else 0].rearrange("p a b -> p (a b)"), "s5cp%d" % (cq % 2))

    if stop_after == "s5pro":
        return early_exit()
    k.dma("sp", WAL[:, :, :], walow_sc.h.rearrange("(a p) c -> p a c", p=P), "c2")
    k.dma("sp", WGU[:, :], wgu_sc[:, :], "c3")
    seq = ORDER_META + ORDER_X * nblk
    st = {"issued": 0, "pos": 0, "xt": 0, "ev": 0, "it": 0}
    dbg = {}

    def tap_(name, ap, n):
        if taps and name in taps and name not in dbg:
            d = k.dram("dbg_" + name, [ap.shape[0], n], F32, "ExternalOutput")
            dbg[name] = d
            k.dma("pool", d[:, :], ap, "tap_" + name)

    def issue_upto(n):
        while st["issued"] < min(n, len(seq)):
            i = st["issued"]
            nm = seq[i]
            slot = i % NSLOT
            dst = RING[:, slot, :]
            if nm in s5chunks:
                k.dma("sp", dst, s5chunks[nm][:, :], "ring%d" % slot)
            else:
                t = chunks[nm][0]
                k.dma("sp", dst.rearrange("p (a c) -> p a c", c=512), t.h.rearrange("(a p) c -> p a c", p=P),
                      "ring%d" % slot)
            st["issued"] += 1

    def wnext(nm):
        i = st["pos"]
        assert seq[i] == nm, (i, seq[i], nm)
        issue_upto(i + NSLOT)
        st["pos"] += 1
        return RING[:, i % NSLOT, :]

    def ev_eng():
        st["ev"] += 1
        return "act" if st["ev"] % 2 else "dve"

    def evcopy(out, in_, eng=None):
        k.copy(eng or ev_eng(), out, in_)

    def proj_fm(w, actv, ntok, evac):
        w3 = w.rearrange("p (a c) -> p a c", c=512)
        for c in range(4):
            b = bank()
            for kt in range(KT):
                k.mm(b[:, 0:ntok], w3[:, kt, 128 * c:128 * c + 128], actv[:, kt, 0:ntok], start=(kt == 0), stop=(kt == KT - 1))
            evac(c, b[:, 0:ntok])

    def proj_tm(w, actv, ntile, evac):
        w3 = w.rearrange("p (a c) -> p a c", c=512)
        for t in range(ntile):
            b = bank()
            for kt in range(KT):
                k.mm(b[:, 0:512], actv[:, kt, 128 * t:128 * t + 128], w3[:, kt, :], start=(kt == 0), stop=(kt == KT - 1))
            evac(t, b[:, 0:512])

    def norm_stats(srcv, nkt, n, ones, out_rstd, presq=False, sqb=None):
        sqb = sq if sqb is None else sqb
        if not presq:
            k.act(sqb[:, 0:nkt, 0:n], srcv, AF.Square)
        b = bank()
        for kt in range(nkt):
            k.mm(b[:, 0:n], ones, sqb[:, kt, 0:n], start=(kt == 0), stop=(kt == nkt - 1))
        k.act(LNT[:, 0:n], b[:, 0:n], AF.Ln, bias=EPS)
        k.act(out_rstd, LNT[:, 0:n], AF.Exp, scale=-0.5)

    def ensure_x(n):
        while st["xt"] < min(n, 4 * nblk):
            g = st["xt"]
            k.dma("act", XIN[:, g % 3, :], x_d[128 * g:128 * g + 128, :], "xin%d" % (g % 3))
            st["xt"] += 1

    Wsb = AF_[:, 8192:12288].rearrange("p (c r g) -> p c r g", r=2, g=32)
    LAt = LA[:, :].rearrange("p (t c) -> p t c", c=512)
    xn, uT, gs, mixb = Bv[0], Bv[1], Bv[1], Bv[1]
    rsb, zgs = Bv[3], Bv[2]
    ARR3 = ARR.rearrange("p (r g) -> p r g", r=2)
    AIS3 = AIS.rearrange("p (r g) -> p r g", r=2)
    XP3 = XP.rearrange("p (r g) -> p r g", r=2)
    XPsw = bass.AP(tensor=CONST.h, offset=768 + 32, ap=[[1280, P], [-32, 2], [1, 32]])
    T13 = RT1.rearrange("p (r g) -> p r g", r=2)
    T23 = RT2.rearrange("p (r g) -> p r g", r=2)

    class StopBuild(Exception):
        pass

    def chk(tag, meta):
        if stop_after == tag and not meta:
            raise StopBuild()

    def block(bi, meta):
        NTOK = 128 if meta else TB
        tap = (lambda *a: None) if meta else tap_
        ntile = NTOK // 128
        nch = NTOK // S5T
        for t in range(ntile):
            if meta:
                xs = XIN[:, 0, :]
                k.memset("pool", xs, 0.0)
                k.dma("pool", XIN[112:128, 0, :], meta_d[:, :], "xin0")
            else:
                g = 4 * bi + t
                ensure_x(g + 1)
                xs = XIN[:, g % 3, :]
            for half in range(2):
                b = bank()
                for q in range(4):
                    kt = 4 * half + q
                    k.tr(b[:, 128 * q:128 * q + 128], xs[:, 128 * kt:128 * kt + 128], identF)
                evcopy(hT[:, 4 * half:4 * half + 4, 128 * t:128 * t + 128],
                       b[:, 0:512].rearrange("p (q c) -> p q c", c=128))
            if not meta:
                ensure_x(4 * bi + t + 4)
        if meta:
            ensure_x(3)
        norm_stats(hT[:, :, 0:NTOK], KT, NTOK, onesD, RS[:, 0, 0:NTOK])
        for kt in range(KT):
            k.stt("dve", xn[:, kt, 0:NTOK], hT[:, kt, 0:NTOK], gpre[:, kt:kt + 1], RS[:, 0, 0:NTOK],
                  ALU.mult, ALU.mult)
        chk("b_norm", meta)
        tap("xn", xn[:, :, :].rearrange("p a b -> p (a b)"), 4096)
        for i in range(2):
            w = wnext("u%d" % i)
            proj_fm(w, xn, NTOK, lambda c, ps, i=i: evcopy(uT[:, 4 * i + c, 0:NTOK], ps))
        chk("b_u", meta)
        tap("uT", uT[:, :, :].rearrange("p a b -> p (a b)"), 4096)
        for cp in range(4):
            w5 = wnext("rb%d" % cp).rearrange("p (c j r m) -> p c j r m", j=8, r=2, m=128)
            b4 = [bank() for _ in range(4)]
            for ctl in range(2):
                ct = 2 * cp + ctl
                u3 = uT[:, ct, 0:NTOK].rearrange("p (c j) -> p c j", j=8)
                for ri in range(2):
                    reg = (2 * ctl + ri) * 64
                    for j in range(8):
                        for pl in range(4):
                            rows = slice(32 * pl, 32 * pl + 32)
                            k.mm(b4[pl][:, reg:reg + nch], w5[rows, ctl, j, ri, :], u3[rows, :, j],
                                 start=(j == 0), stop=(j == 7), tile_position=(32 * pl, 0))
            for pl in range(4):
                srcv = b4[pl][:, 0:256].rearrange("p (c r x) -> p c r x", c=2, r=2)[:, :, :, 0:nch]
                g0 = 8 * cp + pl
                dstv = Wsb[:, 0:nch, :, g0:g0 + 5:4].rearrange("p x r c -> p c r x")
                evcopy(dstv, srcv)
        chk("b_W", meta)
        tap("W", AF_[:, 8192:12288], 4096)
        for c in range(nch):
            if c == 0:
                prev, prevsw = XP3, XPsw
            else:
                prev = Wsb[:, c - 1, :, :]
                prevsw = bass.AP(tensor=AF_.h, offset=8192 + 64 * (c - 1) + 32, ap=[[12288, P], [-32, 2], [1, 32]])
            k.tt("pool", T13, prev, ARR3, ALU.mult)
            k.tt("pool", T23, prevsw, AIS3, ALU.mult)
            k.tt("pool", T13, T13, T23, ALU.add)
            k.tt("pool", Wsb[:, c, :, :], Wsb[:, c, :, :], T13, ALU.add)

        def finish_recurrence():
            if not meta:
                k.copy("pool", Xb[:, :, :, 0], XP3)
                k.copy("dve", Xb[:, :, :, 1:nch], Wsb[:, 0:nch - 1, :, :].rearrange("p c r g -> p r g c"))
            k.copy("pool", XP3, Wsb[:, nch - 1, :, :])
            tap("X", AF_[:, 8192:12288], 4096)
        w = wnext("k")
        if not meta:
            proj_fm(w, xn, NTOK, lambda c, ps: evcopy(kT[:, c, 0:NTOK], ps, "act"))
        proj_tm(w, xn, ntile, lambda t, ps: evcopy(ktok[:, t, :], ps, "act"))
        if not meta:
            w = wnext("q")
            proj_fm(w, xn, NTOK, lambda c, ps: k.act(qT[:, c, 0:NTOK], ps, AF.Copy, scale=float(128 ** -0.5)))
        for i in range(2):
            w = wnext("v%d" % i)
            proj_tm(w, xn, ntile, lambda t, ps, i=i: evcopy(vtok[:, t, 512 * i:512 * i + 512], ps, "act"))
        b = bank()
        for kt in range(KT):
            k.mm(b[0:16, 0:NTOK], WAL[:, kt, :], xn[:, kt, 0:NTOK], start=(kt == 0), stop=(kt == KT - 1))
        evcopy(al17[0:16, 0:NTOK], b[0:16, 0:NTOK], "act")
        for t in range(ntile):
            b = bank()
            k.mm(b[:, 0:512], al17[0:16, 128 * t:128 * t + 128], WGU[:, :], start=True, stop=False)
            k.mm(b[:, 0:512], ones1, BGT[:, :], start=False, stop=True)
            k.act(LAt[:, t, :], b[:, 0:512], AF.Exp, scale=-1.0)
            k.act(LAt[:, t, :], LAt[:, t, :], AF.Ln, bias=1.0)
        chk("b_gin", meta)
        tap("la", LA[:, :], 2048)

        def fm_units(name, actv, evac):
            w3 = wnext(name).rearrange("p (a c) -> p a c", c=512)
            for c in range(4):
                b = bank()
                for kt in range(KT):
                    k.mm(b[:, 0:TB], w3[:, kt, 128 * c:128 * c + 128], actv[:, kt, 0:TB], start=(kt == 0), stop=(kt == KT - 1))
                evac(c, b[:, 0:TB])
                yield

        def rz_units():
            for i in range(2):
                yield from fm_units("r%d" % i, xn, lambda c, ps, i=i: k.act(rsb[:, 4 * i + c, :], ps, AF.Silu))
            for i in range(2):
                yield from fm_units("zg%d" % i, xn, lambda c, ps, i=i: k.act(zgs[:, 4 * i + c, :], ps, AF.Sigmoid))

        fill = None if meta else rz_units()

        def filler(n=1):
            if fill is not None:
                for _ in range(n):
                    next(fill, None)

        for t in range(ntile):
            tok = slice(128 * t, 128 * t + 128)
            b = bank()
            k.mm(b[:, 0:512], TRI[:, 0:128], LAt[:, t, :])
            k.act(ATK[:, :], b[:, 0:512], AF.Exp, scale=-1.0)
            k.tt("dve", KTK[:, t % 2, :], ktok[:, t, :], ATK[:, :], ALU.mult)
            bS = bank(hold=True)
            for h in range(4):
                k.mm(bS[:, 2 * h:2 * h + 2], LAt[:, t, 128 * h:128 * h + 128], TRI[:, 128:130])
            sc8 = GT[:, 1024:1032].rearrange("p (h x) -> p h x", x=2)
            sc2 = GT[:, 1032:1036]
            k.act(GT[:, 1024:1032], bS[:, 0:8], AF.Exp)
            k.tt("dve", sc2, GT[:, 1024:1032:2], GT[:, 1025:1032:2], ALU.mult)
            release(bS)
            SST4 = SST[:, :, :]
            if not meta:
                bE = bank(hold=True)
                for h in range(4):
                    k.mm(bE[:, 128 * h:128 * h + 128], LAt[:, t, 128 * h:128 * h + 128], TRI[:, 0:128])
                A1, A1n = GT[:, 0:512], GT[:, 512:1024]
                k.act(A1, bE[:, 0:512], AF.Exp)
                k.act(A1n, bE[:, 0:512], AF.Exp, scale=-1.0)
                qin4 = GB[:, 0:512].rearrange("p (h c) -> p h c", c=128)
                kin4 = GB[:, 512:1024].rearrange("p (h c) -> p h c", c=128)
                PT4 = GB[:, 1024:1536].rearrange("p (h c) -> p h c", c=128)
                k.tt("dve", qin4, qT[:, :, tok], A1.rearrange("p (h c) -> p h c", c=128), ALU.mult)
                k.tt("dve", kin4, kT[:, :, tok], A1n.rearrange("p (h c) -> p h c", c=128), ALU.mult)
                filler(1)
                bC = bank(hold=True)
                for h in range(4):
                    k.mm(bC[:, 128 * h:128 * h + 128], kin4[:, h, :], qin4[:, h, :])
                k.tt("dve", PT4, bC[:, 0:512].rearrange("p (h c) -> p h c", c=128),
                     maskC.rearrange("p (o c) -> p o c", o=1).broadcast_to([P, 4, 128]), ALU.mult)
                k.tt("dve", SSB[:, :, :], SST4, sc8[:, :, 0:1].broadcast_to([P, 4, 256]), ALU.mult)
                filler(1)
                for hp in range(2):
                    bO = bank()
                    for hh in range(2):
                        h = 2 * hp + hh
                        for vt in range(2):
                            c0 = 256 * hh + 128 * vt
                            k.mm(bO[:, c0:c0 + 128], vtok[:, t, 256 * h + 128 * vt:256 * h + 128 * vt + 128], PT4[:, h, :],
                                 start=True, stop=False)
                            k.mm(bO[:, c0:c0 + 128], SSB[:, h, 128 * vt:128 * vt + 128], qin4[:, h, :], start=False, stop=True)
                    k.copy("act", FA[:, 4 * hp:4 * hp + 4, tok], bO[:, 0:512].rearrange("p (v c) -> p v c", c=128))
            else:
                bE = bank(hold=True)
                bC = bank(hold=True)
            for hp in range(2):
                bD = bE if hp == 0 else bC
                for hh in range(2):
                    h = 2 * hp + hh
                    k.mm(bD[:, 256 * hh:256 * hh + 256], KTK[:, t % 2, 128 * h:128 * h + 128], vtok[:, t, 256 * h:256 * h + 256])
            k.tt("dve", SST4, SST4, sc2.rearrange("p (h o) -> p h o", o=1).broadcast_to([P, 4, 256]), ALU.mult)
            for hp in range(2):
                bD = bE if hp == 0 else bC
                for hh in range(2):
                    h = 2 * hp + hh
                    k.stt("dve", SST[:, h, :], bD[:, 256 * hh:256 * hh + 256], sc8[:, h, 1:2], SST[:, h, :], ALU.mult, ALU.add)
            release(bE)
            release(bC)
            filler(2)
        if fill is not None:
            for _ in fill:
                pass
        if not meta:
            for h in range(4):
                norm_stats(FA[:, 2 * h:2 * h + 2, :], 2, TB, ones256, RS[:, h % 2, :])
                for vt in range(2):
                    ci = 2 * h + vt
                    k.stt("dve", FA[:, ci, :], FA[:, ci, :], ggla[:, ci:ci + 1], RS[:, h % 2, :], ALU.mult, ALU.mult)
                    k.tt("dve", rsb[:, ci, :], FA[:, ci, :], rsb[:, ci, :], ALU.mult)
            for i in range(2):
                w = wnext("wo%d" % i)
                proj_fm(w, rsb, NTOK, lambda c, ps, i=i: k.tt("dve", FA[:, 4 * i + c, :], ps, zgs[:, 4 * i + c, :], ALU.mult))
        finish_recurrence()
        if not meta:
            for hf in range(2):
                wkl = wnext("kl%d" % hf).rearrange("p (c t m) -> p c t m", t=8, m=128)
                yb = [bank() for _ in range(4)]
                for ctl in range(4):
                    ct = 4 * hf + ctl
                    u3 = uT[:, ct, :].rearrange("p (c j) -> p c j", j=8)
                    y3 = yb[ctl][:, 0:512].rearrange("p (c j) -> p c j", j=8)
                    for tau in range(8):
                        k.mm(y3[:, :, tau:8], wkl[:, ctl, tau, :], u3[:, :, 0:8 - tau], start=(tau == 0), stop=False)
                for cq2 in range(2):
                    cq = 2 * hf + cq2
                    wcp = wnext("cp%d" % cq).rearrange("p (g i r m) -> p g i r m", i=8, r=2, m=32)
                    for ctl2 in range(2):
                        ct = 2 * cq + ctl2
                        ctl = ct - 4 * hf
                        b = yb[ctl]
                        y3 = b[:, 0:512].rearrange("p (c j) -> p c j", j=8)
                        for i in range(8):
                            for ri in range(2):
                                for pl in range(4):
                                    gp = 4 * ct + pl
                                    gpl = gp - 8 * cq
                                    k.mm(y3[32 * pl:32 * pl + 32, :, i], wcp[:, gpl, i, ri, :], Xb[:, ri, gp, :],
                                         start=False, stop=(pl == 3 and i == 7 and ri == 1), tile_position=(0, 32 * pl))
                        k.act(gs[:, ct, :], b[:, 0:512], AF.Copy if (taps and "ylin" in taps) else AF.Gelu_apprx_tanh)
            tap("KLc", s5chunks["kl0"][:, :], 4096)
            tap("CPc", s5chunks["cp0"][:, :], 4096)
            tap("gs", gs[:, :, :].rearrange("p a b -> p (a b)"), 4096)
        def proj_fm_units(name, actv, evac):
            w3 = wnext(name).rearrange("p (a c) -> p a c", c=512)
            for c in range(4):
                b = bank()
                for kt in range(KT):
                    k.mm(b[:, 0:TB], w3[:, kt, 128 * c:128 * c + 128], actv[:, kt, 0:TB], start=(kt == 0), stop=(kt == KT - 1))
                evac(c, b[:, 0:TB])
                yield

        def tail_units():
            for i in range(2):
                yield from proj_fm_units("glu%d" % i, gs, lambda c, ps, i=i: k.act(
                    FB[:, 4 * i + c, :], ps, AF.Identity, bias=bglu[:, 4 * i + c:4 * i + c + 1]))
            for i in range(2):
                def ev_b(c, ps, i=i):
                    ci = 4 * i + c
                    tb = sq[:, ci % 4, :]
                    k.act(tb, ps, AF.Sigmoid, bias=bglu[:, 8 + ci:9 + ci])
                    k.tt("pool", FB[:, ci, :], FB[:, ci, :], tb, ALU.mult)
                yield from proj_fm_units("glu%d" % (2 + i), gs, ev_b)
            for i in range(2):
                def ev_zs(c, ps, i=i):
                    ci = 4 * i + c
                    tb = sq[:, 4 + ci % 4, :]
                    k.act(tb, ps, AF.Sigmoid)
                    k.tt("pool", FB[:, ci, :], FB[:, ci, :], tb, ALU.mult)
                    k.tt("dve", mixb[:, ci, :], FB[:, ci, :], FA[:, ci, :], ALU.add)
                yield from proj_fm_units("zs%d" % i, xn, ev_zs)

        units = None if meta else tail_units()

        if meta:
            return
        for _ in units:
            pass
        tap("mixb", AF_[:, 8192:12288], 4096)
        chk("b_gla", meta)
        tap("oT", AF_[:, 4096:8192], 4096)
        chk("b_og", meta)
        tap("og", rsb[:, :, :].rearrange("p a b -> p (a b)"), 4096)
        tap("mix", mixb[:, :, :].rearrange("p a b -> p (a b)"), 4096)
        for i in range(2):
            w = wnext("wout%d" % i)
            def ev_wo(c, ps, i=i):
                ci = 4 * i + c
                k.act(sq[:, ci, :], ps, AF.Square)
                k.copy("dve", FA[:, ci, :], ps)
            proj_fm(w, mixb, NTOK, ev_wo)
        norm_stats(FA[:, :, :], KT, TB, onesD, RS[:, 0, :], presq=True)
        for kt in range(KT):
            k.stt("dve", FA[:, kt, :], FA[:, kt, :], gpost[:, kt:kt + 1], RS[:, 0, :], ALU.mult, ALU.mult)
            k.tt("pool", hT[:, kt, :], hT[:, kt, :], FA[:, kt, :], ALU.add)
            k.act(sq[:, kt, :], hT[:, kt, :], AF.Square)
        chk("b_h1", meta)
        tap("h1", AF_[:, 0:4096], 4096)
        norm_stats(hT[:, :, :], KT, TB, onesD, RS[:, 1, :], presq=True)
        hn = Bv[0]
        for kt in range(KT):
            k.stt("dve", hn[:, kt, :], hT[:, kt, :], gfpre[:, kt:kt + 1], RS[:, 1, :], ALU.mult, ALU.mult)
        for i in range(8):
            w = wnext("ff1_%d" % i)

            def ev_f1(c, ps, i=i):
                ci = 4 * i + c
                tf = ATK[:, :] if ci % 2 else LNT[:, :]
                k.act(tf, ps, AF.Relu)
                k.tt("dve" if ci % 2 else "pool", hid[:, ci, :], tf, tf, ALU.mult)
            proj_fm(w, hn, NTOK, ev_f1)
        for half in range(2):
            b4 = [bank() for _ in range(4)]
            for rc in range(4):
                w3 = wnext("ff2_%d" % (4 * half + rc)).rearrange("p (a c) -> p a c", c=512)
                for c in range(4):
                    for kt in range(KT):
                        k.mm(b4[c][:, 0:512], w3[:, kt, 128 * c:128 * c + 128], hid[:, 8 * rc + kt, :],
                             start=(rc == 0 and kt == 0), stop=(rc == 3 and kt == KT - 1))
            for c in range(4):
                k.act(Bv[0][:, 4 * half + c, :], b4[c][:, 0:512], AF.Square)
                k.copy("dve", FA[:, 4 * half + c, :], b4[c][:, 0:512])
        norm_stats(FA[:, :, :], KT, TB, onesD, RS[:, 0, :], presq=True, sqb=Bv[0])
        for kt in range(KT):
            k.stt("dve", FA[:, kt, :], FA[:, kt, :], gfpost[:, kt:kt + 1], RS[:, 0, :], ALU.mult, ALU.mult)
            k.tt("pool", hT[:, kt, :], hT[:, kt, :], FA[:, kt, :], ALU.add)
        chk("b_ffn", meta)
        for t in range(ntile):
            xo = XOUT[:, t % 2, :]
            for half in range(2):
                b = bank()
                for q in range(4):
                    kt = 4 * half + q
                    k.tr(b[:, 128 * q:128 * q + 128], hT[:, kt, 128 * t:128 * t + 128], identF)
                evcopy(xo[:, 512 * half:512 * half + 512], b[:, 0:512])
            r0 = TB * bi + 128 * t
            k.dma("act", y_d[r0:r0 + 128, :], xo, "xout%d" % (t % 2))

    block(0, True)
    if stop_after == "meta":
        return early_exit()
    try:
        for bi in range(nblk):
            block(bi, False)
    except StopBuild:
        return early_exit()
    assert st["pos"] == len(seq)
    outs = [y_d] + list(dbg.values())
    regs = []
    for o in outs:
        regs.extend(o.regs.values())
    k.S.emit(regs)
    k.es.close()
    return nc, list(dbg.keys())


def _colT(v):
    return np.ascontiguousarray(np.asarray(v, np.float32).reshape(-1, 128).T)


def prep_common(inp):
    f = lambda a: np.ascontiguousarray(np.asarray(a, np.float32))
    pvec = np.concatenate([_colT(inp["g_mix_pre"][0]), _colT(inp["g_mix_post"][0]), _colT(inp["g_ffn_pre"][0]),
                           _colT(inp["g_ffn_post"][0]), _colT(np.asarray(inp["gla_norm_g"][0]).reshape(-1)),
                           _colT(inp["b_glu"][0]), _colT(inp["d_skip"][0])], axis=1)
    assert pvec.shape == (128, 64)
    return {
        "meta": f(inp["meta_tokens"]), "pvec": f(pvec), "w_in": f(inp["w_in"][0]), "w_o": f(inp["w_o_gla"][0]),
        "w_glu": f(inp["w_glu"][0]), "w_out": f(inp["w_out"][0]), "w_ff1": f(inp["w_ff1"][0]),
        "w_ff2": f(inp["w_ff2"][0]), "wgu": f(inp["w_gate_up"][0]), "bgate": f(np.asarray(inp["b_gate"][0])[None, :]),
        "are": f(inp["a_re"][0]), "aim": f(inp["a_im"][0]), "lstep": f(np.asarray(inp["log_step"][0])[:, None]),
        "bre": f(np.asarray(inp["b_re"][0]).reshape(64, 1024)), "bim": f(np.asarray(inp["b_im"][0]).reshape(64, 1024)),
        "cre": f(np.asarray(inp["c_re"][0]).reshape(64, 1024)), "cim": f(np.asarray(inp["c_im"][0]).reshape(64, 1024)),
    }


_CACHE = {}


def kernel(**inputs):
    x = np.asarray(inputs["x"], np.float32)
    bsz, seq, _ = x.shape
    nblk = seq // TB
    if nblk not in _CACHE:
        _CACHE[nblk] = build_program(nblk)[0]
    nc = _CACHE[nblk]
    common = prep_common(inputs)
    in_maps = []
    for b in range(bsz):
        m = dict(common)
        m["x"] = np.ascontiguousarray(x[b])
        in_maps.append(m)
    res = run_bass_kernel_spmd(nc, in_maps, core_ids=list(range(bsz)))
    return np.stack([np.asarray(r["y"], np.float32) for r in res.results], axis=0)
```

```python
import contextlib
import numpy as np
import concourse.bass as bass
import concourse.mybir as mybir
from concourse.bass_utils import run_bass_kernel_spmd

F32 = mybir.dt.float32
BF16 = mybir.dt.bfloat16
AF = mybir.ActivationFunctionType
ALU = mybir.AluOpType

SELF_SYNC = True


class Reg:
    __slots__ = ("w", "r")

    def __init__(self):
        self.w = {}
        self.r = {}


class TT:
    def __init__(self, h, name, sub=None, pstep=None):
        self.h = h
        self.name = name
        self.sub = sub
        self.pstep = pstep
        self.regs = {}

    def __getitem__(self, idx):
        return self.h[idx]

    def regions(self, ap):
        if self.sub is None:
            ks = (0,)
        else:
            off = ap.offset
            dims = list(ap.ap)
            if self.pstep:
                off = off % self.pstep
                dims = dims[1:]
            lo = hi = off
            for st, n in dims:
                if st >= 0:
                    hi += st * (n - 1)
                else:
                    lo += st * (n - 1)
            ks = range(lo // self.sub, hi // self.sub + 1)
        out = []
        for k in ks:
            r = self.regs.get(k)
            if r is None:
                r = self.regs[k] = Reg()
            out.append(r)
        return out


class Op:
    __slots__ = ("eng", "fn", "waits", "flag", "idx", "dma", "semval")

    def __init__(self, eng, fn, dma):
        self.eng = eng
        self.fn = fn
        self.waits = {}
        self.flag = False
        self.dma = dma
        self.semval = None


class Sched:
    ENGS = ("pe", "act", "dve", "pool", "sp")

    def __init__(self, nc):
        self.nc = nc
        self.ops = {e: [] for e in self.ENGS}
        self.dma_ops = []
        self.seen = {e: {} for e in self.ENGS}

    def _dep(self, waits, chan, idx):
        if idx is None:
            return
        if waits.get(chan, -1) < idx:
            waits[chan] = idx

    def op(self, eng, fn, reads=(), writes=(), dma=False):
        o = Op(eng, fn, dma)
        waits = {}
        for t in reads:
            for c, i in t.w.items():
                self._dep(waits, c, i)
        for t in writes:
            for c, i in t.w.items():
                self._dep(waits, c, i)
            for c, i in t.r.items():
                self._dep(waits, c, i)
        seen = self.seen[eng]
        for c, i in waits.items():
            if c == eng and not dma:
                if eng == "pe" or not SELF_SYNC:
                    continue
            if seen.get(c, -1) >= i:
                continue
            seen[c] = i
            o.waits[c] = i
            self._chan_ops(c)[i].flag = True
        if dma:
            chan = ("d", len(self.dma_ops))
            self.dma_ops.append(o)
            o.idx = 0
            o.flag = True
            self.ops[eng].append(o)
        else:
            chan = eng
            o.idx = len(self.ops[eng])
            self.ops[eng].append(o)
        for t in reads:
            t.r[chan] = o.idx
        for t in writes:
            t.w[chan] = o.idx
        return o

    def _chan_ops(self, c):
        if isinstance(c, tuple):
            return [self.dma_ops[c[1]]]
        return self.ops[c]

    def check(self):
        ptr = {e: 0 for e in self.ENGS}
        done_dma = set()
        progress = True
        while progress:
            progress = False
            for e in self.ENGS:
                while ptr[e] < len(self.ops[e]):
                    o = self.ops[e][ptr[e]]
                    ok = True
                    for c, i in o.waits.items():
                        if isinstance(c, tuple):
                            if c not in done_dma:
                                ok = False
                        elif ptr[c] <= i:
                            ok = False
                    if not ok:
                        break
                    if o.dma:
                        done_dma.add(("d", self.dma_ops.index(o)))
                    ptr[e] += 1
                    progress = True
        stuck = {e: (ptr[e], len(self.ops[e])) for e in self.ENGS if ptr[e] < len(self.ops[e])}
        return stuck

    def emit(self, final_waits):
        nc = self.nc
        stuck = self.check()
        assert not stuck, ("DEADLOCK", stuck)
        fin = Op("sp", None, False)
        for t in final_waits:
            for c, i in list(t.w.items()):
                self._dep(fin.waits, c, i)
                self._chan_ops(c)[i].flag = True
        with contextlib.ExitStack() as es:
            esem = {e: es.enter_context(nc.semaphore("s_" + e)) for e in self.ENGS}
            for e in self.ENGS:
                cnt = 0
                for o in self.ops[e]:
                    if o.dma:
                        continue
                    if o.flag:
                        cnt += 1
                        o.semval = cnt
            dsem = {}
            dcnt = {}
            for o in self.dma_ops:
                k = o.fn.semkey
                if k not in dsem:
                    dsem[k] = es.enter_context(nc.semaphore("sd_%s" % (k,)))
                    dcnt[k] = 0
                dcnt[k] += 16
                o.semval = (dsem[k], dcnt[k])

            def wait_list(o):
                res = []
                for c, i in o.waits.items():
                    if isinstance(c, tuple):
                        s, v = self.dma_ops[c[1]].semval
                    else:
                        s, v = esem[c], self.ops[c][i].semval
                    res.append((s, v))
                return res

            block = es.enter_context(nc.Block())

            def run(ename, engobj):
                for o in self.ops[ename]:
                    for s, v in wait_list(o):
                        engobj.wait_ge(s, v)
                    ins = o.fn(engobj)
                    if o.dma:
                        ins.then_inc(o.semval[0], 16)
                    elif o.flag:
                        ins.then_inc(esem[ename], 1)
                if ename == "sp":
                    for s, v in wait_list(fin):
                        engobj.wait_ge(s, v)

            @block.tensor
            def _(e):
                run("pe", e)

            @block.scalar
            def _(e):
                run("act", e)

            @block.vector
            def _(e):
                run("dve", e)

            @block.gpsimd
            def _(e):
                run("pool", e)

            @block.sync
            def _(e):
                run("sp", e)


class DmaFn:
    def __init__(self, semkey, f):
        self.semkey = semkey
        self.f = f

    def __call__(self, e):
        return self.f(e)


class KB:
    def __init__(self, nc):
        self.nc = nc
        self.S = Sched(nc)
        self.es = contextlib.ExitStack()
        self.reg = {}
        self.psum_names = set()

    def _add(self, h, name, sub, pstep):
        t = TT(h, name, sub, pstep)
        self.reg[name] = t
        return t

    def sb(self, name, shape, dt, sub=None):
        h = self.es.enter_context(self.nc.sbuf_tensor(name, list(shape), dt))
        fs = int(np.prod(shape[1:]))
        return self._add(h, name, sub, fs)

    def ps(self, name, shape, dt, sub=None):
        h = self.es.enter_context(self.nc.psum_tensor(name, list(shape), dt))
        self.psum_names.add(name)
        fs = int(np.prod(shape[1:]))
        return self._add(h, name, sub, fs)

    def dram(self, name, shape, dt, kind, sub=None):
        h = self.nc.dram_tensor(name, list(shape), dt, kind=kind)
        return self._add(h.ap(), name, sub, None)

    def regs(self, aps):
        out = []
        for a in aps:
            if a is None or isinstance(a, (int, float)):
                continue
            out.extend(self.reg[a.tensor.name].regions(a))
        return out

    def op(self, eng, fn, outs, ins, dma=False):
        ps_ins = [a for a in ins if a is not None and not isinstance(a, (int, float))
                  and a.tensor.name in self.psum_names]
        return self.S.op(eng, fn, self.regs(ins), self.regs(list(outs) + ps_ins), dma=dma)

    def mm(self, out, lhsT, rhs, start=True, stop=True, **kw):
        return self.op("pe", lambda e: e.matmul(out, lhsT, rhs, start=start, stop=stop, **kw),
                       [out], [lhsT, rhs])

    def tr(self, out, in_, ident):
        return self.op("pe", lambda e: e.transpose(out, in_, ident), [out], [in_, ident])

    def act(self, out, in_, func, bias=0.0, scale=1.0, accum_out=None, eng="act"):
        outs = [out] + ([accum_out] if accum_out is not None else [])
        kw = {}
        if accum_out is not None:
            kw["accum_out"] = accum_out
        return self.op(eng, lambda e: e.activation(out, in_, func, bias=bias, scale=scale, **kw),
                       outs, [in_, bias, scale])

    def tt(self, eng, out, in0, in1, op):
        return self.op(eng, lambda e: e.tensor_tensor(out, in0, in1, op), [out], [in0, in1])

    def ts(self, eng, out, in0, s1, s2, op0, op1=None):
        if op1 is None:
            return self.op(eng, lambda e: e.tensor_scalar(out, in0, s1, None, op0), [out], [in0, s1])
        return self.op(eng, lambda e: e.tensor_scalar(out, in0, s1, s2, op0, op1), [out], [in0, s1, s2])

    def stt(self, eng, out, in0, scalar, in1, op0, op1):
        return self.op(eng, lambda e: e.scalar_tensor_tensor(out, in0, scalar, in1, op0, op1),
                       [out], [in0, scalar, in1])

    def copy(self, eng, out, in_):
        if eng == "act":
            return self.op(eng, lambda e: e.copy(out, in_), [out], [in_])
        return self.op(eng, lambda e: e.tensor_copy(out, in_), [out], [in_])

    def memset(self, eng, ap, val):
        return self.op(eng, lambda e: e.memset(ap, val), [ap], [])

    def dma(self, eng, out, in_, semkey, **kw):
        return self.op(eng, DmaFn(semkey, lambda e: e.dma_start(out, in_, **kw)), [out], [in_], dma=True)

    def finish(self, out_tt):
        regs = []
        for r in out_tt.regs.values():
            regs.append(r)
        self.S.emit(regs)
        self.es.close()


P = 128
D = 1024
KT = 8
TB = 512
S5T = 8
NSLOT = 4
EPS = 1e-6
PI = float(np.pi)

C_Q, C_K, C_V, C_R, C_A, C_U, C_ZG, C_ZS = 0, 512, 1024, 2048, 3072, 3088, 4112, 5136

ORDER_META = ["u0", "u1", "rb0", "rb1", "rb2", "rb3", "k", "v0", "v1"]
ORDER_X = (["u0", "u1", "rb0", "rb1", "rb2", "rb3", "k", "q", "v0", "v1",
            "r0", "r1", "zg0", "zg1",
            "wo0", "wo1",
            "kl0", "cp0", "cp1", "kl1", "cp2", "cp3",
            "glu0", "glu1", "glu2", "glu3", "zs0", "zs1",
            "wout0", "wout1"]
           + ["ff1_%d" % i for i in range(8)] + ["ff2_%d" % i for i in range(8)])


def build_program(nblk, taps=None, stop_after=None):
    nc = bass.Bass("TRN2", target_bir_lowering=False)
    k = KB(nc)
    SEQ = nblk * TB

    x_d = k.dram("x", [SEQ, D], F32, "ExternalInput")
    meta_d = k.dram("meta", [16, D], F32, "ExternalInput")
    pvec_d = k.dram("pvec", [P, 64], F32, "ExternalInput")
    w_in_d = k.dram("w_in", [D, 6160], F32, "ExternalInput")
    w_o_d = k.dram("w_o", [D, D], F32, "ExternalInput")
    w_glu_d = k.dram("w_glu", [D, 2 * D], F32, "ExternalInput")
    w_out_d = k.dram("w_out", [D, D], F32, "ExternalInput")
    w_ff1_d = k.dram("w_ff1", [D, 4 * D], F32, "ExternalInput")
    w_ff2_d = k.dram("w_ff2", [4 * D, D], F32, "ExternalInput")
    wgu_d = k.dram("wgu", [16, 512], F32, "ExternalInput")
    bgate_d = k.dram("bgate", [1, 512], F32, "ExternalInput")
    are_d = k.dram("are", [64, 64], F32, "ExternalInput")
    aim_d = k.dram("aim", [64, 64], F32, "ExternalInput")
    lstep_d = k.dram("lstep", [64, 1], F32, "ExternalInput")
    bre_d = k.dram("bre", [64, 1024], F32, "ExternalInput")
    bim_d = k.dram("bim", [64, 1024], F32, "ExternalInput")
    cre_d = k.dram("cre", [64, 1024], F32, "ExternalInput")
    cim_d = k.dram("cim", [64, 1024], F32, "ExternalInput")
    y_d = k.dram("y", [SEQ, D], F32, "ExternalOutput")

    chunks = {}

    def wchunk(name, src, r0, c0):
        t = k.dram("sc_" + name, [1024, 512], BF16, "Internal")
        chunks[name] = (t, src, r0, c0)

    wchunk("q", w_in_d, 0, C_Q)
    wchunk("k", w_in_d, 0, C_K)
    for i in range(2):
        wchunk("v%d" % i, w_in_d, 0, C_V + 512 * i)
        wchunk("r%d" % i, w_in_d, 0, C_R + 512 * i)
        wchunk("u%d" % i, w_in_d, 0, C_U + 512 * i)
        wchunk("zg%d" % i, w_in_d, 0, C_ZG + 512 * i)
        wchunk("zs%d" % i, w_in_d, 0, C_ZS + 512 * i)
        wchunk("wo%d" % i, w_o_d, 0, 512 * i)
        wchunk("wout%d" % i, w_out_d, 0, 512 * i)
    for i in range(4):
        wchunk("glu%d" % i, w_glu_d, 0, 512 * i)
    for i in range(8):
        wchunk("ff1_%d" % i, w_ff1_d, 0, 512 * i)
    for i in range(8):
        wchunk("ff2_%d" % i, w_ff2_d, 1024 * (i % 4), 512 * (i // 4))
    s5chunks = {}
    for nm in ["rb0", "rb1", "rb2", "rb3", "kl0", "kl1", "cp0", "cp1", "cp2", "cp3"]:
        s5chunks[nm] = k.dram("sc_" + nm, [P, 4096], BF16, "Internal")
    walow_sc = k.dram("sc_alow", [1024, 16], BF16, "Internal")
    wgu_sc = k.dram("sc_wgu", [16, 512], BF16, "Internal")
    Gall_d = k.dram("s5_G", [64, 2, 64, 9, 16], F32, "Internal")
    Hd_d = k.dram("s5_H", [2, 64, 16, 8, 64], F32, "Internal")
    Bd_d = k.dram("s5_B", [64, 2, 64, 16], F32, "Internal")

    AF_ = k.sb("AF32", [P, 12288], F32, sub=512)
    hT = AF_[:, 0:4096].rearrange("p (a b) -> p a b", b=TB)
    FA = AF_[:, 4096:8192].rearrange("p (a b) -> p a b", b=TB)
    FB = AF_[:, 8192:12288].rearrange("p (a b) -> p a b", b=TB)
    LA = k.sb("LA", [P, 2048], F32, sub=128)
    BB = k.sb("BB", [P, 4, 4096], BF16, sub=512)
    Bv = [BB[:, i, :].rearrange("p (a b) -> p a b", b=TB) for i in range(4)]
    AR = k.sb("ARENA", [P, 16384], BF16, sub=512)
    qT = AR[:, 0:2048].rearrange("p (a b) -> p a b", b=TB)
    kT = AR[:, 2048:4096].rearrange("p (a b) -> p a b", b=TB)
    ktok = AR[:, 4096:6144].rearrange("p (a b) -> p a b", b=512)
    vtok = AR[:, 6144:10240].rearrange("p (a b) -> p a b", b=1024)
    Xb = AR[:, 10240:14336].rearrange("p (r g c) -> p r g c", r=2, g=32)
    hid = AR[:, 0:16384].rearrange("p (a b) -> p a b", b=TB)
    sq = AR[:, 12288:16384].rearrange("p (a b) -> p a b", b=TB)
    RING = k.sb("RING", [P, NSLOT, 4096], BF16, sub=4096)
    XIN = k.sb("XIN", [P, 3, 1024], F32, sub=1024)
    XOUT = k.sb("XOUT", [P, 2, 1024], F32, sub=1024)
    SST = k.sb("SST", [P, 4, 256], F32, sub=256)
    SSB = k.sb("SSB", [P, 4, 256], BF16, sub=256)
    GT = k.sb("GT", [P, 1040], F32, sub=512)
    GB = k.sb("GB", [P, 1536], BF16, sub=512)
    KTK = k.sb("KTK", [P, 2, 512], BF16, sub=512)
    ATK = k.sb("ATK", [P, 512], F32)
    RS = k.sb("RSTD", [P, 2, 512], F32, sub=512)
    LNT = k.sb("LNT", [P, 512], F32)
    CONST = k.sb("CONST", [P, 1280], F32, sub=64)
    identF = CONST[:, 0:128]
    TRI = CONST[:, 128:258]
    colR = CONST[:, 258:259]
    pv = CONST[:, 320:384]
    mh = CONST[:, 384:386]
    mga = CONST[:, 386:388]
    bdm = CONST[:, 512:640]
    ARR = CONST[:, 640:704]
    AIS = CONST[:, 704:768]
    XP = CONST[:, 768:832]
    ones1 = CONST[0:1, 896:1024]
    RT1 = CONST[:, 1024:1088]
    RT2 = CONST[:, 1088:1152]
    CB = k.sb("CONSTB", [P, 1024], BF16, sub=128)
    identB = CB[:, 0:128]
    onesD = CB[:, 128:256]
    ones256 = CB[:, 256:384]
    maskC = CB[:, 384:512]
    al17 = CB[0:32, 512:1024]
    WAL = k.sb("WAL", [P, 8, 16], BF16)
    WGU = k.sb("WGU", [16, 512], BF16)
    BGT = k.sb("BGT", [1, 512], F32)
    S5S = LA
    PS = [k.ps("ps%d" % i, [P, 512], F32) for i in range(8)]
    bank_ctr = [0]
    held = set()

    def bank(hold=False):
        for _ in range(8):
            i = bank_ctr[0] % 8
            bank_ctr[0] += 1
            if i not in held:
                if hold:
                    held.add(i)
                return PS[i]
        raise RuntimeError("all PSUM banks held")

    def release(b):
        held.discard(PS.index(b))

    def early_exit():
        regs = []
        for t in k.reg.values():
            if t.pstep is None:
                regs.extend(t.regs.values())
        k.S.emit(regs)
        k.es.close()
        return nc, []

    gpre, gpost, gfpre, gfpost, ggla = (pv[:, 0:8], pv[:, 8:16], pv[:, 16:24], pv[:, 24:32], pv[:, 32:40])
    bglu = pv[:, 40:56]
    dsk = pv[:, 56:64]

    k.dma("sp", pv, pvec_d[:, :], "c0")
    k.dma("sp", BGT[:, :], bgate_d[:, :], "c1")
    k.memset("pool", identF, 1.0)
    k.op("pool", lambda e: e.affine_select(out=identF, in_=identF, compare_op=ALU.is_equal, fill=0.0, base=0,
                                           pattern=[[-1, 128]], channel_multiplier=1), [identF], [identF])
    k.copy("pool", identB, identF)
    k.memset("pool", onesD, 1.0 / 1024.0)
    k.memset("pool", ones256, 1.0 / 256.0)
    k.memset("pool", CONST[0:1, 896:1024], 1.0)
    k.memset("pool", CB[0:32, 512:1024], 1.0)
    k.memset("pool", XP, 0.0)
    k.memset("pool", SST[:, :, :], 0.0)
    k.memset("pool", SSB[:, :, :], 0.0)
    U = GT[:, 0:128]
    k.memset("pool", U, 1.0)
    k.op("pool", lambda e: e.affine_select(out=U, in_=U, compare_op=ALU.is_ge, fill=0.0, base=0,
                                           pattern=[[1, 128]], channel_multiplier=-1), [U], [U])
    k.copy("pool", maskC, U)
    k.memset("pool", colR, 1.0)
    k.op("pool", lambda e: e.affine_select(out=colR, in_=colR, compare_op=ALU.is_ge, fill=0.0, base=64,
                                           pattern=[[0, 1]], channel_multiplier=-1), [colR], [colR])
    k.ts("dve", TRI[:, 0:128], U, colR, -1.0 / 16.0, ALU.subtract, ALU.mult)
    k.ts("dve", TRI[:, 128:129], colR, -1.0 / 16.0, None, ALU.mult)
    k.ts("dve", TRI[:, 129:130], colR, 1.0 / 16.0, -1.0 / 16.0, ALU.mult, ALU.add)
    k.memset("pool", mh, 1.0)
    k.op("pool", lambda e: e.affine_select(out=mh, in_=mh, compare_op=ALU.is_ge, fill=0.0, base=0,
                                           pattern=[[-64, 2]], channel_multiplier=1), [mh], [mh])
    k.op("pool", lambda e: e.affine_select(out=mh, in_=mh, compare_op=ALU.is_ge, fill=0.0, base=63,
                                           pattern=[[64, 2]], channel_multiplier=-1), [mh], [mh])
    bd3 = bdm.rearrange("p (a b) -> p a b", b=16)
    k.memset("pool", bdm, 1.0)
    k.op("pool", lambda e: e.affine_select(out=bd3, in_=bd3, compare_op=ALU.is_ge, fill=0.0, base=0,
                                           pattern=[[-16, 8], [0, 16]], channel_multiplier=1), [bdm], [bdm])
    k.op("pool", lambda e: e.affine_select(out=bd3, in_=bd3, compare_op=ALU.is_ge, fill=0.0, base=15,
                                           pattern=[[16, 8], [0, 16]], channel_multiplier=-1), [bdm], [bdm])
    k.tt("pool", mga[:, 0:1], bdm[:, 0:1], bdm[:, 32:33], ALU.add)
    k.tt("pool", mga[:, 0:1], mga[:, 0:1], bdm[:, 64:65], ALU.add)
    k.tt("pool", mga[:, 0:1], mga[:, 0:1], bdm[:, 96:97], ALU.add)
    k.ts("pool", mga[:, 1:2], mga[:, 0:1], -1.0, 1.0, ALU.mult, ALU.add)

    for ga_ in range(2):
        Sel_ = LA[0:64, 1792 + 32 * ga_:1824 + 32 * ga_]
        k.memset("pool", Sel_, 1.0)
        k.op("pool", lambda e, Sel_=Sel_, ga_=ga_: e.affine_select(out=Sel_, in_=Sel_, compare_op=ALU.is_equal, fill=0.0,
                                                                  base=-ga_, pattern=[[-2, 32]], channel_multiplier=1),
             [Sel_], [Sel_])
    k.dma("pool", walow_sc[:, :], w_in_d[:, C_A:C_A + 16], "cw_alow")
    k.dma("pool", wgu_sc[:, :], wgu_d[:, :], "cw_wgu")
    cast_order = ["u0", "u1", "k", "v0", "v1", "q", "r0", "r1", "zg0", "zg1", "wo0", "wo1", "glu0", "glu1", "glu2",
                  "glu3", "zs0", "zs1", "wout0", "wout1"] + \
                 ["ff1_%d" % i for i in range(8)] + ["ff2_%d" % i for i in range(8)]
    for ci_, nm in enumerate(cast_order):
        t, wsrc, r0, c0 = chunks[nm]
        ins = [wsrc[r0:r0 + 1024, c0:c0 + 512]]
        if ci_ >= 4:
            ins.append(chunks[cast_order[ci_ - 4]][0][:, :])
        k.op("pool", DmaFn("cw_" + nm, lambda e, t=t, wsrc=wsrc, r0=r0, c0=c0: e.dma_start(
            t[:, :], wsrc[r0:r0 + 1024, c0:c0 + 512])), [t[:, :]], ins, dma=True)

    if stop_after == "casts":
        regs = []
        for t in k.reg.values():
            if t.pstep is None:
                regs.extend(t.regs.values())
        k.S.emit(regs)
        k.es.close()
        return nc, []

    def s(a, b):
        return S5S[:, a:b]

    def recip(eng, out, in_):
        return k.op(eng, lambda e: e.reciprocal(out, in_), [out], [in_])

    are, aim, dt, dt16 = s(0, 64), s(64, 128), s(128, 129), s(129, 130)
    lr, li, t1, t2, fr, fi, t3 = s(192, 256), s(256, 320), s(320, 384), s(384, 448), s(448, 512), s(512, 576), s(576, 640)
    LPr, LPi = s(640, 1216), s(1216, 1792)
    Sel0, Sel1 = s(1792, 1824), s(1824, 1856)

    def lpr(kk):
        return S5S[:, 640 + 64 * kk:704 + 64 * kk]

    def lpi(kk):
        return S5S[:, 1216 + 64 * kk:1280 + 64 * kk]

    for hf_ in range(2):
        rows = slice(64 * hf_, 64 * hf_ + 64)
        k.dma("sp", are[rows], are_d[:, :], "c4")
        k.dma("sp", aim[rows], aim_d[:, :], "c5")
        k.dma("sp", dt[rows], lstep_d[:, :], "c6")
    k.act(dt, dt, AF.Exp)
    k.ts("dve", dt16, dt, 1.0 / 16.0, None, ALU.mult)
    k.act(t3, are, AF.Exp, scale=dt16)
    k.ts("dve", t1, aim, dt16, None, ALU.mult)
    k.act(li, t1, AF.Sin)
    k.act(lr, t1, AF.Sin, scale=-1.0, bias=PI / 2)
    k.tt("dve", lr, lr, t3, ALU.mult)
    k.tt("dve", li, li, t3, ALU.mult)
    for _ in range(4):
        k.tt("dve", t1, lr, lr, ALU.mult)
        k.tt("dve", t2, li, li, ALU.mult)
        k.tt("dve", t3, lr, li, ALU.mult)
        k.tt("dve", lr, t1, t2, ALU.subtract)
        k.ts("dve", li, t3, 2.0, None, ALU.mult)
    k.memset("dve", lpr(0), 1.0)
    k.memset("dve", lpi(0), 0.0)
    k.copy("dve", lpr(1), lr)
    k.copy("dve", lpi(1), li)
    for kk in range(2, 9):
        k.tt("dve", t1, lpr(kk - 1), lr, ALU.mult)
        k.tt("dve", t2, lpi(kk - 1), li, ALU.mult)
        k.tt("dve", lpr(kk), t1, t2, ALU.subtract)
        k.tt("dve", t1, lpr(kk - 1), li, ALU.mult)
        k.tt("dve", t2, lpi(kk - 1), lr, ALU.mult)
        k.tt("dve", lpi(kk), t1, t2, ALU.add)
    k.tt("dve", t1, are, are, ALU.mult)
    k.tt("dve", t2, aim, aim, ALU.mult)
    k.tt("dve", t1, t1, t2, ALU.add)
    recip("dve", t3, t1)
    k.ts("dve", t1, lr, -1.0, None, ALU.add)
    k.tt("dve", fr, t1, are, ALU.mult)
    k.tt("dve", t2, li, aim, ALU.mult)
    k.tt("dve", fr, fr, t2, ALU.add)
    k.tt("dve", fr, fr, t3, ALU.mult)
    k.tt("dve", fi, li, are, ALU.mult)
    k.tt("dve", t2, t1, aim, ALU.mult)
    k.tt("dve", fi, fi, t2, ALU.subtract)
    k.tt("dve", fi, fi, t3, ALU.mult)

    if stop_after == "lam":
        return early_exit()
    bA = bank()
    dup = s(320, 448)
    for ri, lp in ((0, lpr(8)), (1, lpi(8))):
        k.copy("dve", dup[0:64, 0:64], lp[0:64])
        k.copy("dve", dup[0:64, 64:128], lp[0:64])
        for ga, Sel in ((0, Sel0), (1, Sel1)):
            k.mm(bA[:, 64 * ga + 32 * ri:64 * ga + 32 * ri + 32], dup[0:64], Sel[0:64])
    for ga in (0, 1):
        rows = slice(64 * ga, 64 * ga + 64)
        k.copy("dve", ARR[rows, 0:32], bA[rows, 64 * ga:64 * ga + 32])
        k.copy("dve", ARR[rows, 32:64], bA[rows, 64 * ga:64 * ga + 32])
        k.ts("dve", AIS[rows, 0:32], bA[rows, 64 * ga + 32:64 * ga + 64], -1.0, None, ALU.mult)
        k.copy("dve", AIS[rows, 32:64], bA[rows, 64 * ga + 32:64 * ga + 64])

    if stop_after == "amat":
        return early_exit()
    bre, bim, tmpA = AF_[:, 0:1024], AF_[:, 1024:2048], AF_[:, 2048:3072]
    for hf_ in range(2):
        rows = slice(64 * hf_, 64 * hf_ + 64)
        k.dma("sp", bre[rows], bre_d[:, :], "c9")
        k.dma("sp", bim[rows], bim_d[:, :], "c10")
    s1, s2 = AF_[:, 3072:4096], AF_[:, 4096:5120]
    fr3 = fr.rearrange("p (n o) -> p n o", o=1).broadcast_to([P, 64, 16])
    fi3 = fi.rearrange("p (n o) -> p n o", o=1).broadcast_to([P, 64, 16])

    def v3(a):
        return a.rearrange("p (n h) -> p n h", h=16)

    k.tt("dve", v3(s1), v3(bre), fr3, ALU.mult)
    k.tt("dve", v3(tmpA), v3(bim), fi3, ALU.mult)
    k.tt("dve", s1, s1, tmpA, ALU.subtract)
    k.tt("dve", v3(s2), v3(bim), fr3, ALU.mult)
    k.tt("dve", v3(tmpA), v3(bre), fi3, ALU.mult)
    k.tt("dve", s2, s2, tmpA, ALU.add)
    k.copy("dve", bre[0:64], s1[0:64])
    k.copy("dve", bim[0:64], s2[0:64])
    k.copy("dve", bre[64:128], s2[64:128])
    k.copy("dve", bim[64:128], s1[64:128])
    k.dma("sp", Bd_d.h.rearrange("g r n h -> g (r n h)"), AF_[0:64, 0:2048], "s5b")
    k.ts("dve", S5S[64:128, 1216:1792], S5S[64:128, 1216:1792], -1.0, None, ALU.mult)
    stageH = AF_[:, 3072:11264].rearrange("p (h j n) -> p h j n", j=8, n=64)
    for j in range(8):
        kk = 7 - j
        ov = stageH[:, :, j, :].rearrange("p h n -> p n h")
        Lr = lpr(kk).rearrange("p (n o) -> p n o", o=1).broadcast_to([P, 64, 16])
        Li = lpi(kk).rearrange("p (n o) -> p n o", o=1).broadcast_to([P, 64, 16])
        k.tt("dve", v3(tmpA), v3(bim), Li, ALU.mult)
        k.tt("dve", ov, v3(bre), Lr, ALU.mult)
        k.tt("dve", ov, ov, v3(tmpA), ALU.subtract)
    for ri in range(2):
        k.dma("sp", Hd_d.h[ri].rearrange("g h j n -> g h (j n)"),
              AF_[64 * ri:64 * ri + 64, 3072:11264].rearrange("p (h x) -> p h x", x=512), "s5h%d" % ri)
    if stop_after == "H":
        return early_exit()

    k.copy("dve", S5S[64:128, 320:448], S5S[64:128, 640:768])
    for c0 in range(0, 576, 64):
        a_ = S5S[64:128, 640 + c0:704 + c0]
        b_ = S5S[64:128, 1216 + c0:1280 + c0]
        t_ = S5S[64:128, 320:384]
        k.copy("dve", t_, a_)
        k.copy("dve", a_, b_)
        k.copy("dve", b_, t_)
    cre, cim = AF_[:, 0:1024], AF_[:, 1024:2048]
    for hf_ in range(2):
        rows = slice(64 * hf_, 64 * hf_ + 64)
        k.dma("sp", cre[rows], cre_d[:, :], "c7")
        k.dma("sp", cim[rows], cim_d[:, :], "c8")
    stageG = AF_[:, 3072:12288].rearrange("p (n k h) -> p n k h", k=9, h=16)
    cre3 = cre.rearrange("p (h n) -> p h n", n=64)
    cim3 = cim.rearrange("p (h n) -> p h n", n=64)
    tmp3 = tmpA.rearrange("p (h n) -> p h n", n=64)
    for kk in range(9):
        ov = stageG[:, :, kk, :].rearrange("p n h -> p h n")
        Lr = lpr(kk).rearrange("p (o n) -> p o n", o=1).broadcast_to([P, 16, 64])
        Li = lpi(kk).rearrange("p (o n) -> p o n", o=1).broadcast_to([P, 16, 64])
        k.tt("dve", tmp3, cim3, Li, ALU.mult)
        k.tt("dve", ov, cre3, Lr, ALU.mult)
        k.tt("dve", ov, ov, tmp3, ALU.subtract)
    for ri in range(2):
        k.dma("sp", Gall_d.h[:, ri, :, :, :].rearrange("g n k h -> g n (k h)"),
              AF_[64 * ri:64 * ri + 64, 3072:12288].rearrange("p (n x) -> p n x", x=144), "s5g%d" % ri)
    if stop_after == "G":
        return early_exit()

    Rm = AF_[:, 0:1024]
    for gq in range(4):
        k.dma("sp", Rm.rearrange("p (g h) -> p g h", h=16)[:, 16 * gq:16 * gq + 16, :],
              Bd_d.h.rearrange("g r n h -> (r n) g h")[:, 16 * gq:16 * gq + 16, :], "s5rm")
    if stop_after == "KLa":
        return early_exit()
    for hf in range(2):
        Lm = AF_[:, 4096:8192].rearrange("p (g t h) -> p g t h", t=8, h=16)
        src = Gall_d.h[32 * hf:32 * hf + 32].rearrange("g r n k h -> (r n) g (k h)")[:, :, 0:128]
        k.dma("sp", AF_[:, 4096:8192].rearrange("p (g x) -> p g x", x=128), src, "s5lm")
        if stop_after == "KLb":
            return early_exit()
        KLst = Bv[0].rearrange("p a b -> p (a b)").rearrange("p (c t m) -> p c t m", t=8, m=128)
        for ctl in range(4):
            ct = 4 * hf + ctl
            for tg in range(2):
                b = bank()
                for tt_ in range(4):
                    tau = 4 * tg + tt_
                    k.mm(b[:, 128 * tt_:128 * tt_ + 128], Rm[:, 128 * ct:128 * ct + 128],
                         Lm[:, 8 * ctl:8 * ctl + 8, tau, :])
                for tt_ in range(4):
                    tau = 4 * tg + tt_
                    if tau == 0:
                        tf = GT[:, 0:128]
                        k.tt("dve", tf, b[:, 0:128], bdm, ALU.mult)
                        k.stt("dve", KLst[:, ctl, 0, :], identF, dsk[:, ct:ct + 1], tf, ALU.mult, ALU.add)
                    else:
                        k.tt("dve", KLst[:, ctl, tau, :], b[:, 128 * tt_:128 * tt_ + 128], bdm, ALU.mult)
        if stop_after in ("KLc", "KLc1", "KLc2", "KLc3"):
            return early_exit()
        k.dma("sp", s5chunks["kl%d" % hf][:, :], Bv[0].rearrange("p a b -> p (a b)"), "s5kl")

    if stop_after == "KL":
        return early_exit()
    for cp in range(4):
        RBst = Bv[1 + cp % 2].rearrange("p a b -> p (a b)").rearrange("p (c j r a n) -> p c j r a n", j=8, r=2, a=2, n=64)
        for ctl in range(2):
            ct = 2 * cp + ctl
            RBf = AF_[:, 1024 + 1024 * (ct % 2):2048 + 1024 * (ct % 2)]
            src = Hd_d.h[:, 8 * ct:8 * ct + 8].rearrange("r g h j n -> (g h) r (j n)")
            k.dma("sp", RBf.rearrange("p (r x) -> p r x", r=2), src, "s5rbf%d" % (ct % 2))
            RBf4 = RBf.rearrange("p (r j n) -> p r j n", r=2, n=64)
            for ri in range(2):
                in0 = RBf4[:, ri, :, :].rearrange("p j (o n) -> p j o n", o=1).broadcast_to([P, 8, 2, 64])
                in1 = mga.rearrange("p (o a q) -> p o a q", o=1, q=1).broadcast_to([P, 8, 2, 64])
                k.tt("dve", RBst[:, ctl, :, ri, :, :], in0, in1, ALU.mult)
        k.dma("sp", s5chunks["rb%d" % cp][:, :], Bv[1 + cp % 2].rearrange("p a b -> p (a b)"), "s5rb%d" % (cp % 2))

    if stop_after == "RB":
        return early_exit()
    CPfs = []
    for cq in range(4):
        CPf = AF_[:, 8192 + 2048 * (cq % 2):10240 + 2048 * (cq % 2)].rearrange("p (g r x) -> p g r x", r=2, x=128)
        CPfs.append(CPf)
    for cq in range(4):
        CPf = CPfs[cq]
        for ga in range(2):
            for ri in range(2):
                src = Gall_d.h.rearrange("(gp a) r n k h -> a n gp r (k h)", a=2)[ga][:, 8 * cq:8 * cq + 8, ri, 16:144]
                k.dma("sp", CPf[64 * ga:64 * ga + 64, :, ri, :], src, "s5cpf%d" % (cq % 2))
        CPst = Bv[3 if cq % 2 else 0].rearrange("p a b -> p (a b)").rearrange("p (g i r a h) -> p g i r a h", i=8, r=2, a=2, h=16)
        for gpl in range(8):
            for ri in range(2):
                in0 = CPf[:, gpl, ri, :].rearrange("p (i o h) -> p i o h", o=1, h=16).broadcast_to([P, 8, 2, 16])
                in1 = mh.rearrange("p (o a q) -> p o a q", o=1, q=1).broadcast_to([P, 8, 2, 16])
                k.tt("dve", CPst[:, gpl, :, ri, :, :], in0, in1, ALU.mult)
        k.dma("sp", s5chunks["cp%d" % cq][:, :], Bv[3 if cq % 2 else 0].rearrange("p a b -> p (a b)"), "s5cp%d" % (cq % 2))

    if stop_after == "s5pro":
        return early_exit()
    k.dma("sp", WAL[:, :, :], walow_sc.h.rearrange("(a p) c -> p a c", p=P), "c2")
    k.dma("sp", WGU[:, :], wgu_sc[:, :], "c3")
    seq = ORDER_META + ORDER_X * nblk
    st = {"issued": 0, "pos": 0, "xt": 0, "ev": 0, "it": 0}
    dbg = {}

    def tap_(name, ap, n):
        if taps and name in taps and name not in dbg:
            d = k.dram("dbg_" + name, [ap.shape[0], n], F32, "ExternalOutput")
            dbg[name] = d
            k.dma("pool", d[:, :], ap, "tap_" + name)

    def issue_upto(n):
        while st["issued"] < min(n, len(seq)):
            i = st["issued"]
            nm = seq[i]
            slot = i % NSLOT
            dst = RING[:, slot, :]
            if nm in s5chunks:
                k.dma("sp", dst, s5chunks[nm][:, :], "ring%d" % slot)
            else:
                t = chunks[nm][0]
                k.dma("sp", dst.rearrange("p (a c) -> p a c", c=512), t.h.rearrange("(a p) c -> p a c", p=P),
                      "ring%d" % slot)
            st["issued"] += 1

    def wnext(nm):
        i = st["pos"]
        assert seq[i] == nm, (i, seq[i], nm)
        issue_upto(i + NSLOT)
        st["pos"] += 1
        return RING[:, i % NSLOT, :]

    def ev_eng():
        st["ev"] += 1
        return "act" if st["ev"] % 2 else "dve"

    def evcopy(out, in_, eng=None):
        k.copy(eng or ev_eng(), out, in_)

    def proj_fm(w, actv, ntok, evac):
        w3 = w.rearrange("p (a c) -> p a c", c=512)
        for c in range(4):
            b = bank()
            for kt in range(KT):
                k.mm(b[:, 0:ntok], w3[:, kt, 128 * c:128 * c + 128], actv[:, kt, 0:ntok], start=(kt == 0), stop=(kt == KT - 1))
            evac(c, b[:, 0:ntok])

    def proj_tm(w, actv, ntile, evac):
        w3 = w.rearrange("p (a c) -> p a c", c=512)
        for t in range(ntile):
            b = bank()
            for kt in range(KT):
                k.mm(b[:, 0:512], actv[:, kt, 128 * t:128 * t + 128], w3[:, kt, :], start=(kt == 0), stop=(kt == KT - 1))
            evac(t, b[:, 0:512])

    def norm_stats(srcv, nkt, n, ones, out_rstd, presq=False, sqb=None):
        sqb = sq if sqb is None else sqb
        if not presq:
            k.act(sqb[:, 0:nkt, 0:n], srcv, AF.Square)
        b = bank()
        for kt in range(nkt):
            k.mm(b[:, 0:n], ones, sqb[:, kt, 0:n], start=(kt == 0), stop=(kt == nkt - 1))
        k.act(LNT[:, 0:n], b[:, 0:n], AF.Ln, bias=EPS)
        k.act(out_rstd, LNT[:, 0:n], AF.Exp, scale=-0.5)

    def ensure_x(n):
        while st["xt"] < min(n, 4 * nblk):
            g = st["xt"]
            k.dma("pool", XIN[:, g % 3, :], x_d[128 * g:128 * g + 128, :], "xin%d" % (g % 3))
            st["xt"] += 1

    Wsb = AF_[:, 8192:12288].rearrange("p (c r g) -> p c r g", r=2, g=32)
    LAt = LA[:, :].rearrange("p (t c) -> p t c", c=512)
    xn, uT, gs, mixb = Bv[0], Bv[1], Bv[1], Bv[1]
    rsb, zgs = Bv[3], Bv[2]
    ARR3 = ARR.rearrange("p (r g) -> p r g", r=2)
    AIS3 = AIS.rearrange("p (r g) -> p r g", r=2)
    XP3 = XP.rearrange("p (r g) -> p r g", r=2)
    XPsw = bass.AP(tensor=CONST.h, offset=768 + 32, ap=[[1280, P], [-32, 2], [1, 32]])
    T13 = RT1.rearrange("p (r g) -> p r g", r=2)
    T23 = RT2.rearrange("p (r g) -> p r g", r=2)

    class StopBuild(Exception):
        pass

    def chk(tag, meta):
        if stop_after == tag and not meta:
            raise StopBuild()

    def block(bi, meta):
        NTOK = 128 if meta else TB
        tap = (lambda *a: None) if meta else tap_
        ntile = NTOK // 128
        nch = NTOK // S5T
        for t in range(ntile):
            if meta:
                xs = XIN[:, 0, :]
                k.memset("pool", xs, 0.0)
                k.dma("pool", XIN[112:128, 0, :], meta_d[:, :], "xin0")
            else:
                g = 4 * bi + t
                ensure_x(g + 1)
                xs = XIN[:, g % 3, :]
            for half in range(2):
                b = bank()
                for q in range(4):
                    kt = 4 * half + q
                    k.tr(b[:, 128 * q:128 * q + 128], xs[:, 128 * kt:128 * kt + 128], identF)
                evcopy(hT[:, 4 * half:4 * half + 4, 128 * t:128 * t + 128],
                       b[:, 0:512].rearrange("p (q c) -> p q c", c=128))
            if not meta:
                ensure_x(4 * bi + t + 4)
        if meta:
            ensure_x(3)
        norm_stats(hT[:, :, 0:NTOK], KT, NTOK, onesD, RS[:, 0, 0:NTOK])
        for kt in range(KT):
            k.stt("dve", xn[:, kt, 0:NTOK], hT[:, kt, 0:NTOK], gpre[:, kt:kt + 1], RS[:, 0, 0:NTOK],
                  ALU.mult, ALU.mult)
        chk("b_norm", meta)
        tap("xn", xn[:, :, :].rearrange("p a b -> p (a b)"), 4096)
        for i in range(2):
            w = wnext("u%d" % i)
            proj_fm(w, xn, NTOK, lambda c, ps, i=i: evcopy(uT[:, 4 * i + c, 0:NTOK], ps))
        chk("b_u", meta)
        tap("uT", uT[:, :, :].rearrange("p a b -> p (a b)"), 4096)
        for cp in range(4):
            w5 = wnext("rb%d" % cp).rearrange("p (c j r m) -> p c j r m", j=8, r=2, m=128)
            b4 = [bank() for _ in range(4)]
            for ctl in range(2):
                ct = 2 * cp + ctl
                u3 = uT[:, ct, 0:NTOK].rearrange("p (c j) -> p c j", j=8)
                for ri in range(2):
                    reg = (2 * ctl + ri) * 64
                    for j in range(8):
                        for pl in range(4):
                            rows = slice(32 * pl, 32 * pl + 32)
                            k.mm(b4[pl][:, reg:reg + nch], w5[rows, ctl, j, ri, :], u3[rows, :, j],
                                 start=(j == 0), stop=(j == 7), tile_position=(32 * pl, 0))
            for pl in range(4):
                srcv = b4[pl][:, 0:256].rearrange("p (c r x) -> p c r x", c=2, r=2)[:, :, :, 0:nch]
                g0 = 8 * cp + pl
                dstv = Wsb[:, 0:nch, :, g0:g0 + 5:4].rearrange("p x r c -> p c r x")
                evcopy(dstv, srcv)
        chk("b_W", meta)
        tap("W", AF_[:, 8192:12288], 4096)
        for c in range(nch):
            if c == 0:
                prev, prevsw = XP3, XPsw
            else:
                prev = Wsb[:, c - 1, :, :]
                prevsw = bass.AP(tensor=AF_.h, offset=8192 + 64 * (c - 1) + 32, ap=[[12288, P], [-32, 2], [1, 32]])
            k.tt("pool", T13, prev, ARR3, ALU.mult)
            k.tt("pool", T23, prevsw, AIS3, ALU.mult)
            k.tt("pool", T13, T13, T23, ALU.add)
            k.tt("pool", Wsb[:, c, :, :], Wsb[:, c, :, :], T13, ALU.add)

        def finish_recurrence():
            if not meta:
                k.copy("pool", Xb[:, :, :, 0], XP3)
                k.copy("dve", Xb[:, :, :, 1:nch], Wsb[:, 0:nch - 1, :, :].rearrange("p c r g -> p r g c"))
            k.copy("pool", XP3, Wsb[:, nch - 1, :, :])
            tap("X", AF_[:, 8192:12288], 4096)
        w = wnext("k")
        if not meta:
            proj_fm(w, xn, NTOK, lambda c, ps: evcopy(kT[:, c, 0:NTOK], ps, "act"))
        proj_tm(w, xn, ntile, lambda t, ps: evcopy(ktok[:, t, :], ps, "act"))
        if not meta:
            w = wnext("q")
            proj_fm(w, xn, NTOK, lambda c, ps: k.act(qT[:, c, 0:NTOK], ps, AF.Copy, scale=float(128 ** -0.5)))
        for i in range(2):
            w = wnext("v%d" % i)
            proj_tm(w, xn, ntile, lambda t, ps, i=i: evcopy(vtok[:, t, 512 * i:512 * i + 512], ps, "act"))
        b = bank()
        for kt in range(KT):
            k.mm(b[0:16, 0:NTOK], WAL[:, kt, :], xn[:, kt, 0:NTOK], start=(kt == 0), stop=(kt == KT - 1))
        evcopy(al17[0:16, 0:NTOK], b[0:16, 0:NTOK], "act")
        for t in range(ntile):
            b = bank()
            k.mm(b[:, 0:512], al17[0:16, 128 * t:128 * t + 128], WGU[:, :], start=True, stop=False)
            k.mm(b[:, 0:512], ones1, BGT[:, :], start=False, stop=True)
            k.act(LAt[:, t, :], b[:, 0:512], AF.Exp, scale=-1.0)
            k.act(LAt[:, t, :], LAt[:, t, :], AF.Ln, bias=1.0)
        chk("b_gin", meta)
        tap("la", LA[:, :], 2048)

        def fm_units(name, actv, evac):
            w3 = wnext(name).rearrange("p (a c) -> p a c", c=512)
            for c in range(4):
                b = bank()
                for kt in range(KT):
                    k.mm(b[:, 0:TB], w3[:, kt, 128 * c:128 * c + 128], actv[:, kt, 0:TB], start=(kt == 0), stop=(kt == KT - 1))
                evac(c, b[:, 0:TB])
                yield

        def rz_units():
            for i in range(2):
                yield from fm_units("r%d" % i, xn, lambda c, ps, i=i: k.act(rsb[:, 4 * i + c, :], ps, AF.Silu))
            for i in range(2):
                yield from fm_units("zg%d" % i, xn, lambda c, ps, i=i: k.act(zgs[:, 4 * i + c, :], ps, AF.Sigmoid))

        fill = None if meta else rz_units()

        def filler(n=1):
            if fill is not None:
                for _ in range(n):
                    next(fill, None)

        for t in range(ntile):
            tok = slice(128 * t, 128 * t + 128)
            b = bank()
            k.mm(b[:, 0:512], TRI[:, 0:128], LAt[:, t, :])
            k.act(ATK[:, :], b[:, 0:512], AF.Exp, scale=-1.0)
            k.tt("dve", KTK[:, t % 2, :], ktok[:, t, :], ATK[:, :], ALU.mult)
            bS = bank(hold=True)
            for h in range(4):
                k.mm(bS[:, 2 * h:2 * h + 2], LAt[:, t, 128 * h:128 * h + 128], TRI[:, 128:130])
            sc8 = GT[:, 1024:1032].rearrange("p (h x) -> p h x", x=2)
            sc2 = GT[:, 1032:1036]
            k.act(GT[:, 1024:1032], bS[:, 0:8], AF.Exp)
            k.tt("dve", sc2, GT[:, 1024:1032:2], GT[:, 1025:1032:2], ALU.mult)
            release(bS)
            SST4 = SST[:, :, :]
            if not meta:
                bE = bank(hold=True)
                for h in range(4):
                    k.mm(bE[:, 128 * h:128 * h + 128], LAt[:, t, 128 * h:128 * h + 128], TRI[:, 0:128])
                A1, A1n = GT[:, 0:512], GT[:, 512:1024]
                k.act(A1, bE[:, 0:512], AF.Exp)
                k.act(A1n, bE[:, 0:512], AF.Exp, scale=-1.0)
                qin4 = GB[:, 0:512].rearrange("p (h c) -> p h c", c=128)
                kin4 = GB[:, 512:1024].rearrange("p (h c) -> p h c", c=128)
                PT4 = GB[:, 1024:1536].rearrange("p (h c) -> p h c", c=128)
                k.tt("dve", qin4, qT[:, :, tok], A1.rearrange("p (h c) -> p h c", c=128), ALU.mult)
                k.tt("dve", kin4, kT[:, :, tok], A1n.rearrange("p (h c) -> p h c", c=128), ALU.mult)
                filler(1)
                bC = bank(hold=True)
                for h in range(4):
                    k.mm(bC[:, 128 * h:128 * h + 128], kin4[:, h, :], qin4[:, h, :])
                k.tt("dve", PT4, bC[:, 0:512].rearrange("p (h c) -> p h c", c=128),
                     maskC.rearrange("p (o c) -> p o c", o=1).broadcast_to([P, 4, 128]), ALU.mult)
                k.tt("dve", SSB[:, :, :], SST4, sc8[:, :, 0:1].broadcast_to([P, 4, 256]), ALU.mult)
                filler(1)
                for hp in range(2):
                    bO = bank()
                    for hh in range(2):
                        h = 2 * hp + hh
                        for vt in range(2):
                            c0 = 256 * hh + 128 * vt
                            k.mm(bO[:, c0:c0 + 128], vtok[:, t, 256 * h + 128 * vt:256 * h + 128 * vt + 128], PT4[:, h, :],
                                 start=True, stop=False)
                            k.mm(bO[:, c0:c0 + 128], SSB[:, h, 128 * vt:128 * vt + 128], qin4[:, h, :], start=False, stop=True)
                    k.copy("act", FA[:, 4 * hp:4 * hp + 4, tok], bO[:, 0:512].rearrange("p (v c) -> p v c", c=128))
            else:
                bE = bank(hold=True)
                bC = bank(hold=True)
            for hp in range(2):
                bD = bE if hp == 0 else bC
                for hh in range(2):
                    h = 2 * hp + hh
                    k.mm(bD[:, 256 * hh:256 * hh + 256], KTK[:, t % 2, 128 * h:128 * h + 128], vtok[:, t, 256 * h:256 * h + 256])
            k.tt("dve", SST4, SST4, sc2.rearrange("p (h o) -> p h o", o=1).broadcast_to([P, 4, 256]), ALU.mult)
            for hp in range(2):
                bD = bE if hp == 0 else bC
                for hh in range(2):
                    h = 2 * hp + hh
                    k.stt("dve", SST[:, h, :], bD[:, 256 * hh:256 * hh + 256], sc8[:, h, 1:2], SST[:, h, :], ALU.mult, ALU.add)
            release(bE)
            release(bC)
            filler(2)
        if fill is not None:
            for _ in fill:
                pass
        if not meta:
            for h in range(4):
                norm_stats(FA[:, 2 * h:2 * h + 2, :], 2, TB, ones256, RS[:, h % 2, :])
                for vt in range(2):
                    ci = 2 * h + vt
                    k.stt("dve", FA[:, ci, :], FA[:, ci, :], ggla[:, ci:ci + 1], RS[:, h % 2, :], ALU.mult, ALU.mult)
                    k.tt("dve", rsb[:, ci, :], FA[:, ci, :], rsb[:, ci, :], ALU.mult)
            for i in range(2):
                w = wnext("wo%d" % i)
                proj_fm(w, rsb, NTOK, lambda c, ps, i=i: k.tt("dve", FA[:, 4 * i + c, :], ps, zgs[:, 4 * i + c, :], ALU.mult))
        finish_recurrence()
        if not meta:
            for hf in range(2):
                wkl = wnext("kl%d" % hf).rearrange("p (c t m) -> p c t m", t=8, m=128)
                yb = [bank() for _ in range(4)]
                for ctl in range(4):
                    ct = 4 * hf + ctl
                    u3 = uT[:, ct, :].rearrange("p (c j) -> p c j", j=8)
                    y3 = yb[ctl][:, 0:512].rearrange("p (c j) -> p c j", j=8)
                    for tau in range(8):
                        k.mm(y3[:, :, tau:8], wkl[:, ctl, tau, :], u3[:, :, 0:8 - tau], start=(tau == 0), stop=False)
                for cq2 in range(2):
                    cq = 2 * hf + cq2
                    wcp = wnext("cp%d" % cq).rearrange("p (g i r m) -> p g i r m", i=8, r=2, m=32)
                    for ctl2 in range(2):
                        ct = 2 * cq + ctl2
                        ctl = ct - 4 * hf
                        b = yb[ctl]
                        y3 = b[:, 0:512].rearrange("p (c j) -> p c j", j=8)
                        for i in range(8):
                            for ri in range(2):
                                for pl in range(4):
                                    gp = 4 * ct + pl
                                    gpl = gp - 8 * cq
                                    k.mm(y3[32 * pl:32 * pl + 32, :, i], wcp[:, gpl, i, ri, :], Xb[:, ri, gp, :],
                                         start=False, stop=(pl == 3 and i == 7 and ri == 1), tile_position=(0, 32 * pl))
                        k.act(gs[:, ct, :], b[:, 0:512], AF.Copy if (taps and "ylin" in taps) else AF.Gelu_apprx_tanh)
            tap("KLc", s5chunks["kl0"][:, :], 4096)
            tap("CPc", s5chunks["cp0"][:, :], 4096)
            tap("gs", gs[:, :, :].rearrange("p a b -> p (a b)"), 4096)
        def proj_fm_units(name, actv, evac):
            w3 = wnext(name).rearrange("p (a c) -> p a c", c=512)
            for c in range(4):
                b = bank()
                for kt in range(KT):
                    k.mm(b[:, 0:TB], w3[:, kt, 128 * c:128 * c + 128], actv[:, kt, 0:TB], start=(kt == 0), stop=(kt == KT - 1))
                evac(c, b[:, 0:TB])
                yield

        def tail_units():
            for i in range(2):
                yield from proj_fm_units("glu%d" % i, gs, lambda c, ps, i=i: k.act(
                    FB[:, 4 * i + c, :], ps, AF.Identity, bias=bglu[:, 4 * i + c:4 * i + c + 1]))
            for i in range(2):
                def ev_b(c, ps, i=i):
                    ci = 4 * i + c
                    tb = sq[:, ci % 4, :]
                    k.act(tb, ps, AF.Sigmoid, bias=bglu[:, 8 + ci:9 + ci])
                    k.tt("pool", FB[:, ci, :], FB[:, ci, :], tb, ALU.mult)
                yield from proj_fm_units("glu%d" % (2 + i), gs, ev_b)
            for i in range(2):
                def ev_zs(c, ps, i=i):
                    ci = 4 * i + c
                    tb = sq[:, 4 + ci % 4, :]
                    k.act(tb, ps, AF.Sigmoid)
                    k.tt("pool", FB[:, ci, :], FB[:, ci, :], tb, ALU.mult)
                    k.tt("dve", mixb[:, ci, :], FB[:, ci, :], FA[:, ci, :], ALU.add)
                yield from proj_fm_units("zs%d" % i, xn, ev_zs)

        units = None if meta else tail_units()

        if meta:
            return
        for _ in units:
            pass
        tap("mixb", AF_[:, 8192:12288], 4096)
        chk("b_gla", meta)
        tap("oT", AF_[:, 4096:8192], 4096)
        chk("b_og", meta)
        tap("og", rsb[:, :, :].rearrange("p a b -> p (a b)"), 4096)
        tap("mix", mixb[:, :, :].rearrange("p a b -> p (a b)"), 4096)
        for i in range(2):
            w = wnext("wout%d" % i)
            def ev_wo(c, ps, i=i):
                ci = 4 * i + c
                k.act(sq[:, ci, :], ps, AF.Square)
                k.copy("dve", FA[:, ci, :], ps)
            proj_fm(w, mixb, NTOK, ev_wo)
        norm_stats(FA[:, :, :], KT, TB, onesD, RS[:, 0, :], presq=True)
        for kt in range(KT):
            k.stt("dve", FA[:, kt, :], FA[:, kt, :], gpost[:, kt:kt + 1], RS[:, 0, :], ALU.mult, ALU.mult)
            k.tt("pool", hT[:, kt, :], hT[:, kt, :], FA[:, kt, :], ALU.add)
            k.act(sq[:, kt, :], hT[:, kt, :], AF.Square)
        chk("b_h1", meta)
        tap("h1", AF_[:, 0:4096], 4096)
        norm_stats(hT[:, :, :], KT, TB, onesD, RS[:, 1, :], presq=True)
        hn = Bv[0]
        for kt in range(KT):
            k.stt("dve", hn[:, kt, :], hT[:, kt, :], gfpre[:, kt:kt + 1], RS[:, 1, :], ALU.mult, ALU.mult)
        for i in range(8):
            w = wnext("ff1_%d" % i)

            def ev_f1(c, ps, i=i):
                ci = 4 * i + c
                tf = ATK[:, :] if ci % 2 else LNT[:, :]
                k.act(tf, ps, AF.Relu)
                k.tt("dve" if ci % 2 else "pool", hid[:, ci, :], tf, tf, ALU.mult)
            proj_fm(w, hn, NTOK, ev_f1)
        for half in range(2):
            b4 = [bank() for _ in range(4)]
            for rc in range(4):
                w3 = wnext("ff2_%d" % (4 * half + rc)).rearrange("p (a c) -> p a c", c=512)
                for c in range(4):
                    for kt in range(KT):
                        k.mm(b4[c][:, 0:512], w3[:, kt, 128 * c:128 * c + 128], hid[:, 8 * rc + kt, :],
                             start=(rc == 0 and kt == 0), stop=(rc == 3 and kt == KT - 1))
            for c in range(4):
                k.act(Bv[0][:, 4 * half + c, :], b4[c][:, 0:512], AF.Square)
                k.copy("dve", FA[:, 4 * half + c, :], b4[c][:, 0:512])
        norm_stats(FA[:, :, :], KT, TB, onesD, RS[:, 0, :], presq=True, sqb=Bv[0])
        for kt in range(KT):
            k.stt("dve", FA[:, kt, :], FA[:, kt, :], gfpost[:, kt:kt + 1], RS[:, 0, :], ALU.mult, ALU.mult)
            k.tt("pool", hT[:, kt, :], hT[:, kt, :], FA[:, kt, :], ALU.add)
        chk("b_ffn", meta)
        for t in range(ntile):
            xo = XOUT[:, t % 2, :]
            for half in range(2):
                b = bank()
                for q in range(4):
                    kt = 4 * half + q
                    k.tr(b[:, 128 * q:128 * q + 128], hT[:, kt, 128 * t:128 * t + 128], identF)
                evcopy(xo[:, 512 * half:512 * half + 512], b[:, 0:512])
            r0 = TB * bi + 128 * t
            k.dma("act", y_d[r0:r0 + 128, :], xo, "xout%d" % (t % 2))

    block(0, True)
    if stop_after == "meta":
        return early_exit()
    try:
        for bi in range(nblk):
            block(bi, False)
    except StopBuild:
        return early_exit()
    assert st["pos"] == len(seq)
    outs = [y_d] + list(dbg.values())
    regs = []
    for o in outs:
        regs.extend(o.regs.values())
    k.S.emit(regs)
    k.es.close()
    return nc, list(dbg.keys())


def _colT(v):
    return np.ascontiguousarray(np.asarray(v, np.float32).reshape(-1, 128).T)


def prep_common(inp):
    f = lambda a: np.ascontiguousarray(np.asarray(a, np.float32))
    pvec = np.concatenate([_colT(inp["g_mix_pre"][0]), _colT(inp["g_mix_post"][0]), _colT(inp["g_ffn_pre"][0]),
                           _colT(inp["g_ffn_post"][0]), _colT(np.asarray(inp["gla_norm_g"][0]).reshape(-1)),
                           _colT(inp["b_glu"][0]), _colT(inp["d_skip"][0])], axis=1)
    assert pvec.shape == (128, 64)
    return {
        "meta": f(inp["meta_tokens"]), "pvec": f(pvec), "w_in": f(inp["w_in"][0]), "w_o": f(inp["w_o_gla"][0]),
        "w_glu": f(inp["w_glu"][0]), "w_out": f(inp["w_out"][0]), "w_ff1": f(inp["w_ff1"][0]),
        "w_ff2": f(inp["w_ff2"][0]), "wgu": f(inp["w_gate_up"][0]), "bgate": f(np.asarray(inp["b_gate"][0])[None, :]),
        "are": f(inp["a_re"][0]), "aim": f(inp["a_im"][0]), "lstep": f(np.asarray(inp["log_step"][0])[:, None]),
        "bre": f(np.asarray(inp["b_re"][0]).reshape(64, 1024)), "bim": f(np.asarray(inp["b_im"][0]).reshape(64, 1024)),
        "cre": f(np.asarray(inp["c_re"][0]).reshape(64, 1024)), "cim": f(np.asarray(inp["c_im"][0]).reshape(64, 1024)),
    }


_CACHE = {}


def kernel(**inputs):
    x = np.asarray(inputs["x"], np.float32)
    bsz, seq, _ = x.shape
    nblk = seq // TB
    if nblk not in _CACHE:
        _CACHE[nblk] = build_program(nblk)[0]
    nc = _CACHE[nblk]
    common = prep_common(inputs)
    in_maps = []
    for b in range(bsz):
        m = dict(common)
        m["x"] = np.ascontiguousarray(x[b])
        in_maps.append(m)
    res = run_bass_kernel_spmd(nc, in_maps, core_ids=list(range(bsz)))
    return np.stack([np.asarray(r["y"], np.float32) for r in res.results], axis=0)
```

```python
import contextlib
import numpy as np
import concourse.bass as bass
import concourse.mybir as mybir
from concourse.bass_utils import run_bass_kernel_spmd

F32 = mybir.dt.float32
BF16 = mybir.dt.bfloat16
AF = mybir.ActivationFunctionType
ALU = mybir.AluOpType

SELF_SYNC = True


class Reg:
    __slots__ = ("w", "r")

    def __init__(self):
        self.w = {}
        self.r = {}


class TT:
    def __init__(self, h, name, sub=None, pstep=None):
        self.h = h
        self.name = name
        self.sub = sub
        self.pstep = pstep
        self.regs = {}

    def __getitem__(self, idx):
        return self.h[idx]

    def regions(self, ap):
        if self.sub is None:
            ks = (0,)
        else:
            off = ap.offset
            dims = list(ap.ap)
            if self.pstep:
                off = off % self.pstep
                dims = dims[1:]
            lo = hi = off
            for st, n in dims:
                if st >= 0:
                    hi += st * (n - 1)
                else:
                    lo += st * (n - 1)
            ks = range(lo // self.sub, hi // self.sub + 1)
        out = []
        for k in ks:
            r = self.regs.get(k)
            if r is None:
                r = self.regs[k] = Reg()
            out.append(r)
        return out


class Op:
    __slots__ = ("eng", "fn", "waits", "flag", "idx", "dma", "semval")

    def __init__(self, eng, fn, dma):
        self.eng = eng
        self.fn = fn
        self.waits = {}
        self.flag = False
        self.dma = dma
        self.semval = None


class Sched:
    ENGS = ("pe", "act", "dve", "pool", "sp")

    def __init__(self, nc):
        self.nc = nc
        self.ops = {e: [] for e in self.ENGS}
        self.dma_ops = []
        self.seen = {e: {} for e in self.ENGS}

    def _dep(self, waits, chan, idx):
        if idx is None:
            return
        if waits.get(chan, -1) < idx:
            waits[chan] = idx

    def op(self, eng, fn, reads=(), writes=(), dma=False):
        o = Op(eng, fn, dma)
        waits = {}
        for t in reads:
            for c, i in t.w.items():
                self._dep(waits, c, i)
        for t in writes:
            for c, i in t.w.items():
                self._dep(waits, c, i)
            for c, i in t.r.items():
                self._dep(waits, c, i)
        seen = self.seen[eng]
        for c, i in waits.items():
            if c == eng and not dma:
                if eng == "pe" or not SELF_SYNC:
                    continue
            if seen.get(c, -1) >= i:
                continue
            seen[c] = i
            o.waits[c] = i
            self._chan_ops(c)[i].flag = True
        if dma:
            chan = ("d", len(self.dma_ops))
            self.dma_ops.append(o)
            o.idx = 0
            o.flag = True
            self.ops[eng].append(o)
        else:
            chan = eng
            o.idx = len(self.ops[eng])
            self.ops[eng].append(o)
        for t in reads:
            t.r[chan] = o.idx
        for t in writes:
            t.w[chan] = o.idx
        return o

    def _chan_ops(self, c):
        if isinstance(c, tuple):
            return [self.dma_ops[c[1]]]
        return self.ops[c]

    def check(self):
        ptr = {e: 0 for e in self.ENGS}
        done_dma = set()
        progress = True
        while progress:
            progress = False
            for e in self.ENGS:
                while ptr[e] < len(self.ops[e]):
                    o = self.ops[e][ptr[e]]
                    ok = True
                    for c, i in o.waits.items():
                        if isinstance(c, tuple):
                            if c not in done_dma:
                                ok = False
                        elif ptr[c] <= i:
                            ok = False
                    if not ok:
                        break
                    if o.dma:
                        done_dma.add(("d", self.dma_ops.index(o)))
                    ptr[e] += 1
                    progress = True
        stuck = {e: (ptr[e], len(self.ops[e])) for e in self.ENGS if ptr[e] < len(self.ops[e])}
        return stuck

    def emit(self, final_waits):
        nc = self.nc
        stuck = self.check()
        assert not stuck, ("DEADLOCK", stuck)
        fin = Op("sp", None, False)
        for t in final_waits:
            for c, i in list(t.w.items()):
                self._dep(fin.waits, c, i)
                self._chan_ops(c)[i].flag = True
        with contextlib.ExitStack() as es:
            esem = {e: es.enter_context(nc.semaphore("s_" + e)) for e in self.ENGS}
            for e in self.ENGS:
                cnt = 0
                for o in self.ops[e]:
                    if o.dma:
                        continue
                    if o.flag:
                        cnt += 1
                        o.semval = cnt
            dsem = {}
            dcnt = {}
            for o in self.dma_ops:
                k = o.fn.semkey
                if k not in dsem:
                    dsem[k] = es.enter_context(nc.semaphore("sd_%s" % (k,)))
                    dcnt[k] = 0
                dcnt[k] += 16
                o.semval = (dsem[k], dcnt[k])

            def wait_list(o):
                res = []
                for c, i in o.waits.items():
                    if isinstance(c, tuple):
                        s, v = self.dma_ops[c[1]].semval
                    else:
                        s, v = esem[c], self.ops[c][i].semval
                    res.append((s, v))
                return res

            block = es.enter_context(nc.Block())

            def run(ename, engobj):
                for o in self.ops[ename]:
                    for s, v in wait_list(o):
                        engobj.wait_ge(s, v)
                    ins = o.fn(engobj)
                    if o.dma:
                        ins.then_inc(o.semval[0], 16)
                    elif o.flag:
                        ins.then_inc(esem[ename], 1)
                if ename == "sp":
                    for s, v in wait_list(fin):
                        engobj.wait_ge(s, v)

            @block.tensor
            def _(e):
                run("pe", e)

            @block.scalar
            def _(e):
                run("act", e)

            @block.vector
            def _(e):
                run("dve", e)

            @block.gpsimd
            def _(e):
                run("pool", e)

            @block.sync
            def _(e):
                run("sp", e)


class DmaFn:
    def __init__(self, semkey, f):
        self.semkey = semkey
        self.f = f

    def __call__(self, e):
        return self.f(e)


class KB:
    def __init__(self, nc):
        self.nc = nc
        self.S = Sched(nc)
        self.es = contextlib.ExitStack()
        self.reg = {}
        self.psum_names = set()

    def _add(self, h, name, sub, pstep):
        t = TT(h, name, sub, pstep)
        self.reg[name] = t
        return t

    def sb(self, name, shape, dt, sub=None):
        h = self.es.enter_context(self.nc.sbuf_tensor(name, list(shape), dt))
        fs = int(np.prod(shape[1:]))
        return self._add(h, name, sub, fs)

    def ps(self, name, shape, dt, sub=None):
        h = self.es.enter_context(self.nc.psum_tensor(name, list(shape), dt))
        self.psum_names.add(name)
        fs = int(np.prod(shape[1:]))
        return self._add(h, name, sub, fs)

    def dram(self, name, shape, dt, kind, sub=None):
        h = self.nc.dram_tensor(name, list(shape), dt, kind=kind)
        return self._add(h.ap(), name, sub, None)

    def regs(self, aps):
        out = []
        for a in aps:
            if a is None or isinstance(a, (int, float)):
                continue
            out.extend(self.reg[a.tensor.name].regions(a))
        return out

    def op(self, eng, fn, outs, ins, dma=False):
        ps_ins = [a for a in ins if a is not None and not isinstance(a, (int, float))
                  and a.tensor.name in self.psum_names]
        return self.S.op(eng, fn, self.regs(ins), self.regs(list(outs) + ps_ins), dma=dma)

    def mm(self, out, lhsT, rhs, start=True, stop=True, **kw):
        return self.op("pe", lambda e: e.matmul(out, lhsT, rhs, start=start, stop=stop, **kw),
                       [out], [lhsT, rhs])

    def tr(self, out, in_, ident):
        return self.op("pe", lambda e: e.transpose(out, in_, ident), [out], [in_, ident])

    def act(self, out, in_, func, bias=0.0, scale=1.0, accum_out=None, eng="act"):
        outs = [out] + ([accum_out] if accum_out is not None else [])
        kw = {}
        if accum_out is not None:
            kw["accum_out"] = accum_out
        return self.op(eng, lambda e: e.activation(out, in_, func, bias=bias, scale=scale, **kw),
                       outs, [in_, bias, scale])

    def tt(self, eng, out, in0, in1, op):
        return self.op(eng, lambda e: e.tensor_tensor(out, in0, in1, op), [out], [in0, in1])

    def ts(self, eng, out, in0, s1, s2, op0, op1=None):
        if op1 is None:
            return self.op(eng, lambda e: e.tensor_scalar(out, in0, s1, None, op0), [out], [in0, s1])
        return self.op(eng, lambda e: e.tensor_scalar(out, in0, s1, s2, op0, op1), [out], [in0, s1, s2])

    def stt(self, eng, out, in0, scalar, in1, op0, op1):
        return self.op(eng, lambda e: e.scalar_tensor_tensor(out, in0, scalar, in1, op0, op1),
                       [out], [in0, scalar, in1])

    def copy(self, eng, out, in_):
        if eng == "act":
            return self.op(eng, lambda e: e.copy(out, in_), [out], [in_])
        return self.op(eng, lambda e: e.tensor_copy(out, in_), [out], [in_])

    def memset(self, eng, ap, val):
        return self.op(eng, lambda e: e.memset(ap, val), [ap], [])

    def dma(self, eng, out, in_, semkey, **kw):
        return self.op(eng, DmaFn(semkey, lambda e: e.dma_start(out, in_, **kw)), [out], [in_], dma=True)

    def finish(self, out_tt):
        regs = []
        for r in out_tt.regs.values():
            regs.append(r)
        self.S.emit(regs)
        self.es.close()


P = 128
D = 1024
KT = 8
TB = 512
S5T = 8
NSLOT = 4
EPS = 1e-6
PI = float(np.pi)

C_Q, C_K, C_V, C_R, C_A, C_U, C_ZG, C_ZS = 0, 512, 1024, 2048, 3072, 3088, 4112, 5136

ORDER_META = ["u0", "u1", "rb0", "rb1", "rb2", "rb3", "k", "v0", "v1"]
ORDER_X = (["u0", "u1", "rb0", "rb1", "rb2", "rb3", "k", "q", "v0", "v1",
            "r0", "r1", "zg0", "zg1",
            "wo0", "wo1",
            "kl0", "cp0", "cp1", "kl1", "cp2", "cp3",
            "glu0", "glu1", "glu2", "glu3", "zs0", "zs1",
            "wout0", "wout1"]
           + ["ff1_%d" % i for i in range(8)] + ["ff2_%d" % i for i in range(8)])


def build_program(nblk, taps=None, stop_after=None):
    nc = bass.Bass("TRN2", target_bir_lowering=False)
    k = KB(nc)
    SEQ = nblk * TB

    x_d = k.dram("x", [SEQ, D], F32, "ExternalInput")
    meta_d = k.dram("meta", [16, D], F32, "ExternalInput")
    pvec_d = k.dram("pvec", [P, 64], F32, "ExternalInput")
    w_in_d = k.dram("w_in", [D, 6160], F32, "ExternalInput")
    w_o_d = k.dram("w_o", [D, D], F32, "ExternalInput")
    w_glu_d = k.dram("w_glu", [D, 2 * D], F32, "ExternalInput")
    w_out_d = k.dram("w_out", [D, D], F32, "ExternalInput")
    w_ff1_d = k.dram("w_ff1", [D, 4 * D], F32, "ExternalInput")
    w_ff2_d = k.dram("w_ff2", [4 * D, D], F32, "ExternalInput")
    wgu_d = k.dram("wgu", [16, 512], F32, "ExternalInput")
    bgate_d = k.dram("bgate", [1, 512], F32, "ExternalInput")
    are_d = k.dram("are", [64, 64], F32, "ExternalInput")
    aim_d = k.dram("aim", [64, 64], F32, "ExternalInput")
    lstep_d = k.dram("lstep", [64, 1], F32, "ExternalInput")
    bre_d = k.dram("bre", [64, 1024], F32, "ExternalInput")
    bim_d = k.dram("bim", [64, 1024], F32, "ExternalInput")
    cre_d = k.dram("cre", [64, 1024], F32, "ExternalInput")
    cim_d = k.dram("cim", [64, 1024], F32, "ExternalInput")
    y_d = k.dram("y", [SEQ, D], F32, "ExternalOutput")

    chunks = {}

    def wchunk(name, src, r0, c0):
        t = k.dram("sc_" + name, [1024, 512], BF16, "Internal")
        chunks[name] = (t, src, r0, c0)

    wchunk("q", w_in_d, 0, C_Q)
    wchunk("k", w_in_d, 0, C_K)
    for i in range(2):
        wchunk("v%d" % i, w_in_d, 0, C_V + 512 * i)
        wchunk("r%d" % i, w_in_d, 0, C_R + 512 * i)
        wchunk("u%d" % i, w_in_d, 0, C_U + 512 * i)
        wchunk("zg%d" % i, w_in_d, 0, C_ZG + 512 * i)
        wchunk("zs%d" % i, w_in_d, 0, C_ZS + 512 * i)
        wchunk("wo%d" % i, w_o_d, 0, 512 * i)
        wchunk("wout%d" % i, w_out_d, 0, 512 * i)
    for i in range(4):
        wchunk("glu%d" % i, w_glu_d, 0, 512 * i)
    for i in range(8):
        wchunk("ff1_%d" % i, w_ff1_d, 0, 512 * i)
    for i in range(8):
        wchunk("ff2_%d" % i, w_ff2_d, 1024 * (i % 4), 512 * (i // 4))
    s5chunks = {}
    for nm in ["rb0", "rb1", "rb2", "rb3", "kl0", "kl1", "cp0", "cp1", "cp2", "cp3"]:
        s5chunks[nm] = k.dram("sc_" + nm, [P, 4096], BF16, "Internal")
    walow_sc = k.dram("sc_alow", [1024, 16], BF16, "Internal")
    wgu_sc = k.dram("sc_wgu", [16, 512], BF16, "Internal")
    Gall_d = k.dram("s5_G", [64, 2, 64, 9, 16], F32, "Internal")
    Hd_d = k.dram("s5_H", [2, 64, 16, 8, 64], F32, "Internal")
    Bd_d = k.dram("s5_B", [64, 2, 64, 16], F32, "Internal")

    AF_ = k.sb("AF32", [P, 12288], F32, sub=512)
    hT = AF_[:, 0:4096].rearrange("p (a b) -> p a b", b=TB)
    FA = AF_[:, 4096:8192].rearrange("p (a b) -> p a b", b=TB)
    FB = AF_[:, 8192:12288].rearrange("p (a b) -> p a b", b=TB)
    LA = k.sb("LA", [P, 2048], F32, sub=128)
    BB = k.sb("BB", [P, 4, 4096], BF16, sub=512)
    Bv = [BB[:, i, :].rearrange("p (a b) -> p a b", b=TB) for i in range(4)]
    AR = k.sb("ARENA", [P, 16384], BF16, sub=512)
    qT = AR[:, 0:2048].rearrange("p (a b) -> p a b", b=TB)
    kT = AR[:, 2048:4096].rearrange("p (a b) -> p a b", b=TB)
    ktok = AR[:, 4096:6144].rearrange("p (a b) -> p a b", b=512)
    vtok = AR[:, 6144:10240].rearrange("p (a b) -> p a b", b=1024)
    Xb = AR[:, 10240:14336].rearrange("p (r g c) -> p r g c", r=2, g=32)
    hid = AR[:, 0:16384].rearrange("p (a b) -> p a b", b=TB)
    sq = AR[:, 12288:16384].rearrange("p (a b) -> p a b", b=TB)
    RING = k.sb("RING", [P, NSLOT, 4096], BF16, sub=4096)
    XIN = k.sb("XIN", [P, 3, 1024], F32, sub=1024)
    XOUT = k.sb("XOUT", [P, 2, 1024], F32, sub=1024)
    SST = k.sb("SST", [P, 4, 256], F32, sub=256)
    SSB = k.sb("SSB", [P, 4, 256], BF16, sub=256)
    GT = k.sb("GT", [P, 1040], F32, sub=512)
    GB = k.sb("GB", [P, 1536], BF16, sub=512)
    KTK = k.sb("KTK", [P, 2, 512], BF16, sub=512)
    ATK = k.sb("ATK", [P, 512], F32)
    RS = k.sb("RSTD", [P, 2, 512], F32, sub=512)
    LNT = k.sb("LNT", [P, 512], F32)
    CONST = k.sb("CONST", [P, 1280], F32, sub=64)
    identF = CONST[:, 0:128]
    TRI = CONST[:, 128:258]
    colR = CONST[:, 258:259]
    pv = CONST[:, 320:384]
    mh = CONST[:, 384:386]
    mga = CONST[:, 386:388]
    bdm = CONST[:, 512:640]
    ARR = CONST[:, 640:704]
    AIS = CONST[:, 704:768]
    XP = CONST[:, 768:832]
    ones1 = CONST[0:1, 896:1024]
    RT1 = CONST[:, 1024:1088]
    RT2 = CONST[:, 1088:1152]
    CB = k.sb("CONSTB", [P, 1024], BF16, sub=128)
    identB = CB[:, 0:128]
    onesD = CB[:, 128:256]
    ones256 = CB[:, 256:384]
    maskC = CB[:, 384:512]
    al17 = CB[0:32, 512:1024]
    WAL = k.sb("WAL", [P, 8, 16], BF16)
    WGU = k.sb("WGU", [16, 512], BF16)
    BGT = k.sb("BGT", [1, 512], F32)
    S5S = LA
    PS = [k.ps("ps%d" % i, [P, 512], F32) for i in range(8)]
    bank_ctr = [0]
    held = set()

    def bank(hold=False):
        for _ in range(8):
            i = bank_ctr[0] % 8
            bank_ctr[0] += 1
            if i not in held:
                if hold:
                    held.add(i)
                return PS[i]
        raise RuntimeError("all PSUM banks held")

    def release(b):
        held.discard(PS.index(b))

    def early_exit():
        regs = []
        for t in k.reg.values():
            if t.pstep is None:
                regs.extend(t.regs.values())
        k.S.emit(regs)
        k.es.close()
        return nc, []

    gpre, gpost, gfpre, gfpost, ggla = (pv[:, 0:8], pv[:, 8:16], pv[:, 16:24], pv[:, 24:32], pv[:, 32:40])
    bglu = pv[:, 40:56]
    dsk = pv[:, 56:64]

    k.dma("sp", pv, pvec_d[:, :], "c0")
    k.dma("sp", BGT[:, :], bgate_d[:, :], "c1")
    k.memset("pool", identF, 1.0)
    k.op("pool", lambda e: e.affine_select(out=identF, in_=identF, compare_op=ALU.is_equal, fill=0.0, base=0,
                                           pattern=[[-1, 128]], channel_multiplier=1), [identF], [identF])
    k.copy("pool", identB, identF)
    k.memset("pool", onesD, 1.0 / 1024.0)
    k.memset("pool", ones256, 1.0 / 256.0)
    k.memset("pool", CONST[0:1, 896:1024], 1.0)
    k.memset("pool", CB[0:32, 512:1024], 1.0)
    k.memset("pool", XP, 0.0)
    k.memset("pool", SST[:, :, :], 0.0)
    k.memset("pool", SSB[:, :, :], 0.0)
    U = GT[:, 0:128]
    k.memset("pool", U, 1.0)
    k.op("pool", lambda e: e.affine_select(out=U, in_=U, compare_op=ALU.is_ge, fill=0.0, base=0,
                                           pattern=[[1, 128]], channel_multiplier=-1), [U], [U])
    k.copy("pool", maskC, U)
    k.memset("pool", colR, 1.0)
    k.op("pool", lambda e: e.affine_select(out=colR, in_=colR, compare_op=ALU.is_ge, fill=0.0, base=64,
                                           pattern=[[0, 1]], channel_multiplier=-1), [colR], [colR])
    k.ts("dve", TRI[:, 0:128], U, colR, -1.0 / 16.0, ALU.subtract, ALU.mult)
    k.ts("dve", TRI[:, 128:129], colR, -1.0 / 16.0, None, ALU.mult)
    k.ts("dve", TRI[:, 129:130], colR, 1.0 / 16.0, -1.0 / 16.0, ALU.mult, ALU.add)
    k.memset("pool", mh, 1.0)
    k.op("pool", lambda e: e.affine_select(out=mh, in_=mh, compare_op=ALU.is_ge, fill=0.0, base=0,
                                           pattern=[[-64, 2]], channel_multiplier=1), [mh], [mh])
    k.op("pool", lambda e: e.affine_select(out=mh, in_=mh, compare_op=ALU.is_ge, fill=0.0, base=63,
                                           pattern=[[64, 2]], channel_multiplier=-1), [mh], [mh])
    bd3 = bdm.rearrange("p (a b) -> p a b", b=16)
    k.memset("pool", bdm, 1.0)
    k.op("pool", lambda e: e.affine_select(out=bd3, in_=bd3, compare_op=ALU.is_ge, fill=0.0, base=0,
                                           pattern=[[-16, 8], [0, 16]], channel_multiplier=1), [bdm], [bdm])
    k.op("pool", lambda e: e.affine_select(out=bd3, in_=bd3, compare_op=ALU.is_ge, fill=0.0, base=15,
                                           pattern=[[16, 8], [0, 16]], channel_multiplier=-1), [bdm], [bdm])
    k.tt("pool", mga[:, 0:1], bdm[:, 0:1], bdm[:, 32:33], ALU.add)
    k.tt("pool", mga[:, 0:1], mga[:, 0:1], bdm[:, 64:65], ALU.add)
    k.tt("pool", mga[:, 0:1], mga[:, 0:1], bdm[:, 96:97], ALU.add)
    k.ts("pool", mga[:, 1:2], mga[:, 0:1], -1.0, 1.0, ALU.mult, ALU.add)

    for ga_ in range(2):
        Sel_ = LA[0:64, 1792 + 32 * ga_:1824 + 32 * ga_]
        k.memset("pool", Sel_, 1.0)
        k.op("pool", lambda e, Sel_=Sel_, ga_=ga_: e.affine_select(out=Sel_, in_=Sel_, compare_op=ALU.is_equal, fill=0.0,
                                                                  base=-ga_, pattern=[[-2, 32]], channel_multiplier=1),
             [Sel_], [Sel_])
    k.dma("pool", walow_sc[:, :], w_in_d[:, C_A:C_A + 16], "cw_alow")
    k.dma("pool", wgu_sc[:, :], wgu_d[:, :], "cw_wgu")
    cast_order = ["u0", "u1", "k", "v0", "v1", "q", "r0", "r1", "zg0", "zg1", "wo0", "wo1", "glu0", "glu1", "glu2",
                  "glu3", "zs0", "zs1", "wout0", "wout1"] + \
                 ["ff1_%d" % i for i in range(8)] + ["ff2_%d" % i for i in range(8)]
    for ci_, nm in enumerate(cast_order):
        t, wsrc, r0, c0 = chunks[nm]
        ins = [wsrc[r0:r0 + 1024, c0:c0 + 512]]
        if ci_ >= 4:
            ins.append(chunks[cast_order[ci_ - 4]][0][:, :])
        k.op("pool", DmaFn("cw_" + nm, lambda e, t=t, wsrc=wsrc, r0=r0, c0=c0: e.dma_start(
            t[:, :], wsrc[r0:r0 + 1024, c0:c0 + 512])), [t[:, :]], ins, dma=True)

    if stop_after == "casts":
        regs = []
        for t in k.reg.values():
            if t.pstep is None:
                regs.extend(t.regs.values())
        k.S.emit(regs)
        k.es.close()
        return nc, []

    def s(a, b):
        return S5S[:, a:b]

    def recip(eng, out, in_):
        return k.op(eng, lambda e: e.reciprocal(out, in_), [out], [in_])

    are, aim, dt, dt16 = s(0, 64), s(64, 128), s(128, 129), s(129, 130)
    lr, li, t1, t2, fr, fi, t3 = s(192, 256), s(256, 320), s(320, 384), s(384, 448), s(448, 512), s(512, 576), s(576, 640)
    LPr, LPi = s(640, 1216), s(1216, 1792)
    Sel0, Sel1 = s(1792, 1824), s(1824, 1856)

    def lpr(kk):
        return S5S[:, 640 + 64 * kk:704 + 64 * kk]

    def lpi(kk):
        return S5S[:, 1216 + 64 * kk:1280 + 64 * kk]

    for hf_ in range(2):
        rows = slice(64 * hf_, 64 * hf_ + 64)
        k.dma("sp", are[rows], are_d[:, :], "c4")
        k.dma("sp", aim[rows], aim_d[:, :], "c5")
        k.dma("sp", dt[rows], lstep_d[:, :], "c6")
    k.act(dt, dt, AF.Exp)
    k.ts("dve", dt16, dt, 1.0 / 16.0, None, ALU.mult)
    k.act(t3, are, AF.Exp, scale=dt16)
    k.ts("dve", t1, aim, dt16, None, ALU.mult)
    k.act(li, t1, AF.Sin)
    k.act(lr, t1, AF.Sin, scale=-1.0, bias=PI / 2)
    k.tt("dve", lr, lr, t3, ALU.mult)
    k.tt("dve", li, li, t3, ALU.mult)
    for _ in range(4):
        k.tt("dve", t1, lr, lr, ALU.mult)
        k.tt("dve", t2, li, li, ALU.mult)
        k.tt("dve", t3, lr, li, ALU.mult)
        k.tt("dve", lr, t1, t2, ALU.subtract)
        k.ts("dve", li, t3, 2.0, None, ALU.mult)
    k.memset("dve", lpr(0), 1.0)
    k.memset("dve", lpi(0), 0.0)
    k.copy("dve", lpr(1), lr)
    k.copy("dve", lpi(1), li)
    for kk in range(2, 9):
        k.tt("dve", t1, lpr(kk - 1), lr, ALU.mult)
        k.tt("dve", t2, lpi(kk - 1), li, ALU.mult)
        k.tt("dve", lpr(kk), t1, t2, ALU.subtract)
        k.tt("dve", t1, lpr(kk - 1), li, ALU.mult)
        k.tt("dve", t2, lpi(kk - 1), lr, ALU.mult)
        k.tt("dve", lpi(kk), t1, t2, ALU.add)
    k.tt("dve", t1, are, are, ALU.mult)
    k.tt("dve", t2, aim, aim, ALU.mult)
    k.tt("dve", t1, t1, t2, ALU.add)
    recip("dve", t3, t1)
    k.ts("dve", t1, lr, -1.0, None, ALU.add)
    k.tt("dve", fr, t1, are, ALU.mult)
    k.tt("dve", t2, li, aim, ALU.mult)
    k.tt("dve", fr, fr, t2, ALU.add)
    k.tt("dve", fr, fr, t3, ALU.mult)
    k.tt("dve", fi, li, are, ALU.mult)
    k.tt("dve", t2, t1, aim, ALU.mult)
    k.tt("dve", fi, fi, t2, ALU.subtract)
    k.tt("dve", fi, fi, t3, ALU.mult)

    if stop_after == "lam":
        return early_exit()
    bA = bank()
    dup = s(320, 448)
    for ri, lp in ((0, lpr(8)), (1, lpi(8))):
        k.copy("dve", dup[0:64, 0:64], lp[0:64])
        k.copy("dve", dup[0:64, 64:128], lp[0:64])
        for ga, Sel in ((0, Sel0), (1, Sel1)):
            k.mm(bA[:, 64 * ga + 32 * ri:64 * ga + 32 * ri + 32], dup[0:64], Sel[0:64])
    for ga in (0, 1):
        rows = slice(64 * ga, 64 * ga + 64)
        k.copy("dve", ARR[rows, 0:32], bA[rows, 64 * ga:64 * ga + 32])
        k.copy("dve", ARR[rows, 32:64], bA[rows, 64 * ga:64 * ga + 32])
        k.ts("dve", AIS[rows, 0:32], bA[rows, 64 * ga + 32:64 * ga + 64], -1.0, None, ALU.mult)
        k.copy("dve", AIS[rows, 32:64], bA[rows, 64 * ga + 32:64 * ga + 64])

    if stop_after == "amat":
        return early_exit()
    bre, bim, tmpA = AF_[:, 0:1024], AF_[:, 1024:2048], AF_[:, 2048:3072]
    for hf_ in range(2):
        rows = slice(64 * hf_, 64 * hf_ + 64)
        k.dma("sp", bre[rows], bre_d[:, :], "c9")
        k.dma("sp", bim[rows], bim_d[:, :], "c10")
    s1, s2 = AF_[:, 3072:4096], AF_[:, 4096:5120]
    fr3 = fr.rearrange("p (n o) -> p n o", o=1).broadcast_to([P, 64, 16])
    fi3 = fi.rearrange("p (n o) -> p n o", o=1).broadcast_to([P, 64, 16])

    def v3(a):
        return a.rearrange("p (n h) -> p n h", h=16)

    k.tt("dve", v3(s1), v3(bre), fr3, ALU.mult)
    k.tt("dve", v3(tmpA), v3(bim), fi3, ALU.mult)
    k.tt("dve", s1, s1, tmpA, ALU.subtract)
    k.tt("dve", v3(s2), v3(bim), fr3, ALU.mult)
    k.tt("dve", v3(tmpA), v3(bre), fi3, ALU.mult)
    k.tt("dve", s2, s2, tmpA, ALU.add)
    k.copy("dve", bre[0:64], s1[0:64])
    k.copy("dve", bim[0:64], s2[0:64])
    k.copy("dve", bre[64:128], s2[64:128])
    k.copy("dve", bim[64:128], s1[64:128])
    k.dma("sp", Bd_d.h.rearrange("g r n h -> g (r n h)"), AF_[0:64, 0:2048], "s5b")
    k.ts("dve", S5S[64:128, 1216:1792], S5S[64:128, 1216:1792], -1.0, None, ALU.mult)
    stageH = AF_[:, 3072:11264].rearrange("p (h j n) -> p h j n", j=8, n=64)
    for j in range(8):
        kk = 7 - j
        ov = stageH[:, :, j, :].rearrange("p h n -> p n h")
        Lr = lpr(kk).rearrange("p (n o) -> p n o", o=1).broadcast_to([P, 64, 16])
        Li = lpi(kk).rearrange("p (n o) -> p n o", o=1).broadcast_to([P, 64, 16])
        k.tt("dve", v3(tmpA), v3(bim), Li, ALU.mult)
        k.tt("dve", ov, v3(bre), Lr, ALU.mult)
        k.tt("dve", ov, ov, v3(tmpA), ALU.subtract)
    for ri in range(2):
        k.dma("sp", Hd_d.h[ri].rearrange("g h j n -> g h (j n)"),
              AF_[64 * ri:64 * ri + 64, 3072:11264].rearrange("p (h x) -> p h x", x=512), "s5h%d" % ri)
    if stop_after == "H":
        return early_exit()

    k.copy("dve", S5S[64:128, 320:448], S5S[64:128, 640:768])
    for c0 in range(0, 576, 64):
        a_ = S5S[64:128, 640 + c0:704 + c0]
        b_ = S5S[64:128, 1216 + c0:1280 + c0]
        t_ = S5S[64:128, 320:384]
        k.copy("dve", t_, a_)
        k.copy("dve", a_, b_)
        k.copy("dve", b_, t_)
    cre, cim = AF_[:, 0:1024], AF_[:, 1024:2048]
    for hf_ in range(2):
        rows = slice(64 * hf_, 64 * hf_ + 64)
        k.dma("sp", cre[rows], cre_d[:, :], "c7")
        k.dma("sp", cim[rows], cim_d[:, :], "c8")
    stageG = AF_[:, 3072:12288].rearrange("p (n k h) -> p n k h", k=9, h=16)
    cre3 = cre.rearrange("p (h n) -> p h n", n=64)
    cim3 = cim.rearrange("p (h n) -> p h n", n=64)
    tmp3 = tmpA.rearrange("p (h n) -> p h n", n=64)
    for kk in range(9):
        ov = stageG[:, :, kk, :].rearrange("p n h -> p h n")
        Lr = lpr(kk).rearrange("p (o n) -> p o n", o=1).broadcast_to([P, 16, 64])
        Li = lpi(kk).rearrange("p (o n) -> p o n", o=1).broadcast_to([P, 16, 64])
        k.tt("dve", tmp3, cim3, Li, ALU.mult)
        k.tt("dve", ov, cre3, Lr, ALU.mult)
        k.tt("dve", ov, ov, tmp3, ALU.subtract)
    for ri in range(2):
        k.dma("sp", Gall_d.h[:, ri, :, :, :].rearrange("g n k h -> g n (k h)"),
              AF_[64 * ri:64 * ri + 64, 3072:12288].rearrange("p (n x) -> p n x", x=144), "s5g%d" % ri)
    if stop_after == "G":
        return early_exit()

    Rm = AF_[:, 0:1024]
    for gq in range(4):
        k.dma("sp", Rm.rearrange("p (g h) -> p g h", h=16)[:, 16 * gq:16 * gq + 16, :],
              Bd_d.h.rearrange("g r n h -> (r n) g h")[:, 16 * gq:16 * gq + 16, :], "s5rm")
    if stop_after == "KLa":
        return early_exit()
    for hf in range(2):
        Lm = AF_[:, 4096:8192].rearrange("p (g t h) -> p g t h", t=8, h=16)
        src = Gall_d.h[32 * hf:32 * hf + 32].rearrange("g r n k h -> (r n) g (k h)")[:, :, 0:128]
        k.dma("sp", AF_[:, 4096:8192].rearrange("p (g x) -> p g x", x=128), src, "s5lm")
        if stop_after == "KLb":
            return early_exit()
        KLst = Bv[0].rearrange("p a b -> p (a b)").rearrange("p (c t m) -> p c t m", t=8, m=128)
        for ctl in range(4):
            ct = 4 * hf + ctl
            for tg in range(2):
                b = bank()
                for tt_ in range(4):
                    tau = 4 * tg + tt_
                    k.mm(b[:, 128 * tt_:128 * tt_ + 128], Rm[:, 128 * ct:128 * ct + 128],
                         Lm[:, 8 * ctl:8 * ctl + 8, tau, :])
                for tt_ in range(4):
                    tau = 4 * tg + tt_
                    if tau == 0:
                        tf = GT[:, 0:128]
                        k.tt("dve", tf, b[:, 0:128], bdm, ALU.mult)
                        k.stt("dve", KLst[:, ctl, 0, :], identF, dsk[:, ct:ct + 1], tf, ALU.mult, ALU.add)
                    else:
                        k.tt("dve", KLst[:, ctl, tau, :], b[:, 128 * tt_:128 * tt_ + 128], bdm, ALU.mult)
        if stop_after in ("KLc", "KLc1", "KLc2", "KLc3"):
            return early_exit()
        k.dma("sp", s5chunks["kl%d" % hf][:, :], Bv[0].rearrange("p a b -> p (a b)"), "s5kl")

    if stop_after == "KL":
        return early_exit()
    for cp in range(4):
        RBst = Bv[1 + cp % 2].rearrange("p a b -> p (a b)").rearrange("p (c j r a n) -> p c j r a n", j=8, r=2, a=2, n=64)
        for ctl in range(2):
            ct = 2 * cp + ctl
            RBf = AF_[:, 1024 + 1024 * (ct % 2):2048 + 1024 * (ct % 2)]
            src = Hd_d.h[:, 8 * ct:8 * ct + 8].rearrange("r g h j n -> (g h) r (j n)")
            k.dma("sp", RBf.rearrange("p (r x) -> p r x", r=2), src, "s5rbf%d" % (ct % 2))
            RBf4 = RBf.rearrange("p (r j n) -> p r j n", r=2, n=64)
            for ri in range(2):
                in0 = RBf4[:, ri, :, :].rearrange("p j (o n) -> p j o n", o=1).broadcast_to([P, 8, 2, 64])
                in1 = mga.rearrange("p (o a q) -> p o a q", o=1, q=1).broadcast_to([P, 8, 2, 64])
                k.tt("dve", RBst[:, ctl, :, ri, :, :], in0, in1, ALU.mult)
        k.dma("sp", s5chunks["rb%d" % cp][:, :], Bv[1 + cp % 2].rearrange("p a b -> p (a b)"), "s5rb%d" % (cp % 2))

    if stop_after == "RB":
        return early_exit()
    CPfs = []
    for cq in range(4):
        CPf = AF_[:, 8192 + 2048 * (cq % 2):10240 + 2048 * (cq % 2)].rearrange("p (g r x) -> p g r x", r=2, x=128)
        CPfs.append(CPf)
    for cq in range(4):
        CPf = CPfs[cq]
        for ga in range(2):
            for ri in range(2):
                src = Gall_d.h.rearrange("(gp a) r n k h -> a n gp r (k h)", a=2)[ga][:, 8 * cq:8 * cq + 8, ri, 16:144]
                k.dma("sp", CPf[64 * ga:64 * ga + 64, :, ri, :], src, "s5cpf%d" % (cq % 2))
        CPst = Bv[3 if cq % 2 else 0].rearrange("p a b -> p (a b)").rearrange("p (g i r a h) -> p g i r a h", i=8, r=2, a=2, h=16)
        for gpl in range(8):
            for ri in range(2):
                in0 = CPf[:, gpl, ri, :].rearrange("p (i o h) -> p i o h", o=1, h=16).broadcast_to([P, 8, 2, 16])
                in1 = mh.rearrange("p (o a q) -> p o a q", o=1, q=1).broadcast_to([P, 8, 2, 16])
                k.tt("dve", CPst[:, gpl, :, ri, :, :], in0, in1, ALU.mult)
        k.dma("sp", s5chunks["cp%d" % cq][:, :], Bv[3 if cq % 2 else 0].rearrange("p a b -> p (a b)"), "s5cp%d" % (cq % 2))

    if stop_after == "s5pro":
        return early_exit()
    k.dma("sp", WAL[:, :, :], walow_sc.h.rearrange("(a p) c -> p a c", p=P), "c2")
    k.dma("sp", WGU[:, :], wgu_sc[:, :], "c3")
    seq = ORDER_META + ORDER_X * nblk
    st = {"issued": 0, "pos": 0, "xt": 0, "ev": 0, "it": 0}
    dbg = {}

    def tap_(name, ap, n):
        if taps and name in taps and name not in dbg:
            d = k.dram("dbg_" + name, [ap.shape[0], n], F32, "ExternalOutput")
            dbg[name] = d
            k.dma("pool", d[:, :], ap, "tap_" + name)

    def issue_upto(n):
        while st["issued"] < min(n, len(seq)):
            i = st["issued"]
            nm = seq[i]
            slot = i % NSLOT
            dst = RING[:, slot, :]
            if nm in s5chunks:
                k.dma("sp", dst, s5chunks[nm][:, :], "ring%d" % slot)
            else:
                t = chunks[nm][0]
                k.dma("sp", dst.rearrange("p (a c) -> p a c", c=512), t.h.rearrange("(a p) c -> p a c", p=P),
                      "ring%d" % slot)
            st["issued"] += 1

    def wnext(nm):
        i = st["pos"]
        assert seq[i] == nm, (i, seq[i], nm)
        issue_upto(i + NSLOT)
        st["pos"] += 1
        return RING[:, i % NSLOT, :]

    def ev_eng():
        st["ev"] += 1
        return "act" if st["ev"] % 2 else "dve"

    def evcopy(out, in_, eng=None):
        k.copy(eng or ev_eng(), out, in_)

    def proj_fm(w, actv, ntok, evac):
        w3 = w.rearrange("p (a c) -> p a c", c=512)
        for c in range(4):
            b = bank()
            for kt in range(KT):
                k.mm(b[:, 0:ntok], w3[:, kt, 128 * c:128 * c + 128], actv[:, kt, 0:ntok], start=(kt == 0), stop=(kt == KT - 1))
            evac(c, b[:, 0:ntok])

    def proj_tm(w, actv, ntile, evac):
        w3 = w.rearrange("p (a c) -> p a c", c=512)
        for t in range(ntile):
            b = bank()
            for kt in range(KT):
                k.mm(b[:, 0:512], actv[:, kt, 128 * t:128 * t + 128], w3[:, kt, :], start=(kt == 0), stop=(kt == KT - 1))
            evac(t, b[:, 0:512])

    def norm_stats(srcv, nkt, n, ones, out_rstd, presq=False, sqb=None):
        sqb = sq if sqb is None else sqb
        if not presq:
            k.act(sqb[:, 0:nkt, 0:n], srcv, AF.Square)
        b = bank()
        for kt in range(nkt):
            k.mm(b[:, 0:n], ones, sqb[:, kt, 0:n], start=(kt == 0), stop=(kt == nkt - 1))
        k.act(LNT[:, 0:n], b[:, 0:n], AF.Ln, bias=EPS)
        k.act(out_rstd, LNT[:, 0:n], AF.Exp, scale=-0.5)

    def ensure_x(n):
        while st["xt"] < min(n, 4 * nblk):
            g = st["xt"]
            k.dma("sp", XIN[:, g % 3, :], x_d[128 * g:128 * g + 128, :], "xin%d" % (g % 3))
            st["xt"] += 1

    Wsb = AF_[:, 8192:12288].rearrange("p (c r g) -> p c r g", r=2, g=32)
    LAt = LA[:, :].rearrange("p (t c) -> p t c", c=512)
    xn, uT, gs, mixb = Bv[0], Bv[1], Bv[1], Bv[1]
    rsb, zgs = Bv[3], Bv[2]
    ARR3 = ARR.rearrange("p (r g) -> p r g", r=2)
    AIS3 = AIS.rearrange("p (r g) -> p r g", r=2)
    XP3 = XP.rearrange("p (r g) -> p r g", r=2)
    XPsw = bass.AP(tensor=CONST.h, offset=768 + 32, ap=[[1280, P], [-32, 2], [1, 32]])
    T13 = RT1.rearrange("p (r g) -> p r g", r=2)
    T23 = RT2.rearrange("p (r g) -> p r g", r=2)

    class StopBuild(Exception):
        pass

    def chk(tag, meta):
        if stop_after == tag and not meta:
            raise StopBuild()

    def block(bi, meta):
        NTOK = 128 if meta else TB
        tap = (lambda *a: None) if meta else tap_
        ntile = NTOK // 128
        nch = NTOK // S5T
        for t in range(ntile):
            if meta:
                xs = XIN[:, 0, :]
                k.memset("pool", xs, 0.0)
                k.dma("pool", XIN[112:128, 0, :], meta_d[:, :], "xin0")
            else:
                g = 4 * bi + t
                ensure_x(g + 1)
                xs = XIN[:, g % 3, :]
            for half in range(2):
                b = bank()
                for q in range(4):
                    kt = 4 * half + q
                    k.tr(b[:, 128 * q:128 * q + 128], xs[:, 128 * kt:128 * kt + 128], identF)
                evcopy(hT[:, 4 * half:4 * half + 4, 128 * t:128 * t + 128],
                       b[:, 0:512].rearrange("p (q c) -> p q c", c=128))
            if not meta:
                ensure_x(4 * bi + t + 4)
        if meta:
            ensure_x(3)
        norm_stats(hT[:, :, 0:NTOK], KT, NTOK, onesD, RS[:, 0, 0:NTOK])
        for kt in range(KT):
            k.stt("dve", xn[:, kt, 0:NTOK], hT[:, kt, 0:NTOK], gpre[:, kt:kt + 1], RS[:, 0, 0:NTOK],
                  ALU.mult, ALU.mult)
        chk("b_norm", meta)
        tap("xn", xn[:, :, :].rearrange("p a b -> p (a b)"), 4096)
        for i in range(2):
            w = wnext("u%d" % i)
            proj_fm(w, xn, NTOK, lambda c, ps, i=i: evcopy(uT[:, 4 * i + c, 0:NTOK], ps))
        chk("b_u", meta)
        tap("uT", uT[:, :, :].rearrange("p a b -> p (a b)"), 4096)
        for cp in range(4):
            w5 = wnext("rb%d" % cp).rearrange("p (c j r m) -> p c j r m", j=8, r=2, m=128)
            b4 = [bank() for _ in range(4)]
            for ctl in range(2):
                ct = 2 * cp + ctl
                u3 = uT[:, ct, 0:NTOK].rearrange("p (c j) -> p c j", j=8)
                for ri in range(2):
                    reg = (2 * ctl + ri) * 64
                    for j in range(8):
                        for pl in range(4):
                            rows = slice(32 * pl, 32 * pl + 32)
                            k.mm(b4[pl][:, reg:reg + nch], w5[rows, ctl, j, ri, :], u3[rows, :, j],
                                 start=(j == 0), stop=(j == 7), tile_position=(32 * pl, 0))
            for pl in range(4):
                srcv = b4[pl][:, 0:256].rearrange("p (c r x) -> p c r x", c=2, r=2)[:, :, :, 0:nch]
                g0 = 8 * cp + pl
                dstv = Wsb[:, 0:nch, :, g0:g0 + 5:4].rearrange("p x r c -> p c r x")
                evcopy(dstv, srcv)
        chk("b_W", meta)
        tap("W", AF_[:, 8192:12288], 4096)
        for c in range(nch):
            if c == 0:
                prev, prevsw = XP3, XPsw
            else:
                prev = Wsb[:, c - 1, :, :]
                prevsw = bass.AP(tensor=AF_.h, offset=8192 + 64 * (c - 1) + 32, ap=[[12288, P], [-32, 2], [1, 32]])
            k.tt("pool", T13, prev, ARR3, ALU.mult)
            k.tt("pool", T23, prevsw, AIS3, ALU.mult)
            k.tt("pool", T13, T13, T23, ALU.add)
            k.tt("pool", Wsb[:, c, :, :], Wsb[:, c, :, :], T13, ALU.add)

        def finish_recurrence():
            if not meta:
                k.copy("pool", Xb[:, :, :, 0], XP3)
                k.copy("dve", Xb[:, :, :, 1:nch], Wsb[:, 0:nch - 1, :, :].rearrange("p c r g -> p r g c"))
            k.copy("pool", XP3, Wsb[:, nch - 1, :, :])
            tap("X", AF_[:, 8192:12288], 4096)
        w = wnext("k")
        if not meta:
            proj_fm(w, xn, NTOK, lambda c, ps: evcopy(kT[:, c, 0:NTOK], ps, "act"))
        proj_tm(w, xn, ntile, lambda t, ps: evcopy(ktok[:, t, :], ps, "act"))
        if not meta:
            w = wnext("q")
            proj_fm(w, xn, NTOK, lambda c, ps: k.act(qT[:, c, 0:NTOK], ps, AF.Copy, scale=float(128 ** -0.5)))
        for i in range(2):
            w = wnext("v%d" % i)
            proj_tm(w, xn, ntile, lambda t, ps, i=i: evcopy(vtok[:, t, 512 * i:512 * i + 512], ps, "act"))
        b = bank()
        for kt in range(KT):
            k.mm(b[0:16, 0:NTOK], WAL[:, kt, :], xn[:, kt, 0:NTOK], start=(kt == 0), stop=(kt == KT - 1))
        evcopy(al17[0:16, 0:NTOK], b[0:16, 0:NTOK], "act")
        for t in range(ntile):
            b = bank()
            k.mm(b[:, 0:512], al17[0:16, 128 * t:128 * t + 128], WGU[:, :], start=True, stop=False)
            k.mm(b[:, 0:512], ones1, BGT[:, :], start=False, stop=True)
            k.act(LAt[:, t, :], b[:, 0:512], AF.Exp, scale=-1.0)
            k.act(LAt[:, t, :], LAt[:, t, :], AF.Ln, bias=1.0)
        chk("b_gin", meta)
        tap("la", LA[:, :], 2048)

        def fm_units(name, actv, evac):
            w3 = wnext(name).rearrange("p (a c) -> p a c", c=512)
            for c in range(4):
                b = bank()
                for kt in range(KT):
                    k.mm(b[:, 0:TB], w3[:, kt, 128 * c:128 * c + 128], actv[:, kt, 0:TB], start=(kt == 0), stop=(kt == KT - 1))
                evac(c, b[:, 0:TB])
                yield

        def rz_units():
            for i in range(2):
                yield from fm_units("r%d" % i, xn, lambda c, ps, i=i: k.act(rsb[:, 4 * i + c, :], ps, AF.Silu))
            for i in range(2):
                yield from fm_units("zg%d" % i, xn, lambda c, ps, i=i: k.act(zgs[:, 4 * i + c, :], ps, AF.Sigmoid))

        fill = None if meta else rz_units()

        def filler(n=1):
            if fill is not None:
                for _ in range(n):
                    next(fill, None)

        for t in range(ntile):
            tok = slice(128 * t, 128 * t + 128)
            b = bank()
            k.mm(b[:, 0:512], TRI[:, 0:128], LAt[:, t, :])
            k.act(ATK[:, :], b[:, 0:512], AF.Exp, scale=-1.0)
            k.tt("dve", KTK[:, t % 2, :], ktok[:, t, :], ATK[:, :], ALU.mult)
            bS = bank(hold=True)
            for h in range(4):
                k.mm(bS[:, 2 * h:2 * h + 2], LAt[:, t, 128 * h:128 * h + 128], TRI[:, 128:130])
            sc8 = GT[:, 1024:1032].rearrange("p (h x) -> p h x", x=2)
            sc2 = GT[:, 1032:1036]
            k.act(GT[:, 1024:1032], bS[:, 0:8], AF.Exp)
            k.tt("dve", sc2, GT[:, 1024:1032:2], GT[:, 1025:1032:2], ALU.mult)
            release(bS)
            SST4 = SST[:, :, :]
            if not meta:
                bE = bank(hold=True)
                for h in range(4):
                    k.mm(bE[:, 128 * h:128 * h + 128], LAt[:, t, 128 * h:128 * h + 128], TRI[:, 0:128])
                A1, A1n = GT[:, 0:512], GT[:, 512:1024]
                k.act(A1, bE[:, 0:512], AF.Exp)
                k.act(A1n, bE[:, 0:512], AF.Exp, scale=-1.0)
                qin4 = GB[:, 0:512].rearrange("p (h c) -> p h c", c=128)
                kin4 = GB[:, 512:1024].rearrange("p (h c) -> p h c", c=128)
                PT4 = GB[:, 1024:1536].rearrange("p (h c) -> p h c", c=128)
                k.tt("dve", qin4, qT[:, :, tok], A1.rearrange("p (h c) -> p h c", c=128), ALU.mult)
                k.tt("dve", kin4, kT[:, :, tok], A1n.rearrange("p (h c) -> p h c", c=128), ALU.mult)
                filler(1)
                bC = bank(hold=True)
                for h in range(4):
                    k.mm(bC[:, 128 * h:128 * h + 128], kin4[:, h, :], qin4[:, h, :])
                k.tt("dve", PT4, bC[:, 0:512].rearrange("p (h c) -> p h c", c=128),
                     maskC.rearrange("p (o c) -> p o c", o=1).broadcast_to([P, 4, 128]), ALU.mult)
                k.tt("dve", SSB[:, :, :], SST4, sc8[:, :, 0:1].broadcast_to([P, 4, 256]), ALU.mult)
                filler(1)
                for hp in range(2):
                    bO = bank()
                    for hh in range(2):
                        h = 2 * hp + hh
                        for vt in range(2):
                            c0 = 256 * hh + 128 * vt
                            k.mm(bO[:, c0:c0 + 128], vtok[:, t, 256 * h + 128 * vt:256 * h + 128 * vt + 128], PT4[:, h, :],
                                 start=True, stop=False)
                            k.mm(bO[:, c0:c0 + 128], SSB[:, h, 128 * vt:128 * vt + 128], qin4[:, h, :], start=False, stop=True)
                    k.copy("act", FA[:, 4 * hp:4 * hp + 4, tok], bO[:, 0:512].rearrange("p (v c) -> p v c", c=128))
            else:
                bE = bank(hold=True)
                bC = bank(hold=True)
            for hp in range(2):
                bD = bE if hp == 0 else bC
                for hh in range(2):
                    h = 2 * hp + hh
                    k.mm(bD[:, 256 * hh:256 * hh + 256], KTK[:, t % 2, 128 * h:128 * h + 128], vtok[:, t, 256 * h:256 * h + 256])
            k.tt("dve", SST4, SST4, sc2.rearrange("p (h o) -> p h o", o=1).broadcast_to([P, 4, 256]), ALU.mult)
            for hp in range(2):
                bD = bE if hp == 0 else bC
                for hh in range(2):
                    h = 2 * hp + hh
                    k.stt("dve", SST[:, h, :], bD[:, 256 * hh:256 * hh + 256], sc8[:, h, 1:2], SST[:, h, :], ALU.mult, ALU.add)
            release(bE)
            release(bC)
            filler(2)
        if fill is not None:
            for _ in fill:
                pass
        if not meta:
            for h in range(4):
                norm_stats(FA[:, 2 * h:2 * h + 2, :], 2, TB, ones256, RS[:, h % 2, :])
                for vt in range(2):
                    ci = 2 * h + vt
                    k.stt("dve", FA[:, ci, :], FA[:, ci, :], ggla[:, ci:ci + 1], RS[:, h % 2, :], ALU.mult, ALU.mult)
                    k.tt("dve", rsb[:, ci, :], FA[:, ci, :], rsb[:, ci, :], ALU.mult)
            for i in range(2):
                w = wnext("wo%d" % i)
                proj_fm(w, rsb, NTOK, lambda c, ps, i=i: k.tt("dve", FA[:, 4 * i + c, :], ps, zgs[:, 4 * i + c, :], ALU.mult))
        finish_recurrence()
        if not meta:
            for hf in range(2):
                wkl = wnext("kl%d" % hf).rearrange("p (c t m) -> p c t m", t=8, m=128)
                yb = [bank() for _ in range(4)]
                for ctl in range(4):
                    ct = 4 * hf + ctl
                    u3 = uT[:, ct, :].rearrange("p (c j) -> p c j", j=8)
                    y3 = yb[ctl][:, 0:512].rearrange("p (c j) -> p c j", j=8)
                    for tau in range(8):
                        k.mm(y3[:, :, tau:8], wkl[:, ctl, tau, :], u3[:, :, 0:8 - tau], start=(tau == 0), stop=False)
                for cq2 in range(2):
                    cq = 2 * hf + cq2
                    wcp = wnext("cp%d" % cq).rearrange("p (g i r m) -> p g i r m", i=8, r=2, m=32)
                    for ctl2 in range(2):
                        ct = 2 * cq + ctl2
                        ctl = ct - 4 * hf
                        b = yb[ctl]
                        y3 = b[:, 0:512].rearrange("p (c j) -> p c j", j=8)
                        for i in range(8):
                            for ri in range(2):
                                for pl in range(4):
                                    gp = 4 * ct + pl
                                    gpl = gp - 8 * cq
                                    k.mm(y3[32 * pl:32 * pl + 32, :, i], wcp[:, gpl, i, ri, :], Xb[:, ri, gp, :],
                                         start=False, stop=(pl == 3 and i == 7 and ri == 1), tile_position=(0, 32 * pl))
                        k.act(gs[:, ct, :], b[:, 0:512], AF.Copy if (taps and "ylin" in taps) else AF.Gelu_apprx_tanh)
            tap("KLc", s5chunks["kl0"][:, :], 4096)
            tap("CPc", s5chunks["cp0"][:, :], 4096)
            tap("gs", gs[:, :, :].rearrange("p a b -> p (a b)"), 4096)
        def proj_fm_units(name, actv, evac):
            w3 = wnext(name).rearrange("p (a c) -> p a c", c=512)
            for c in range(4):
                b = bank()
                for kt in range(KT):
                    k.mm(b[:, 0:TB], w3[:, kt, 128 * c:128 * c + 128], actv[:, kt, 0:TB], start=(kt == 0), stop=(kt == KT - 1))
                evac(c, b[:, 0:TB])
                yield

        def tail_units():
            for i in range(2):
                yield from proj_fm_units("glu%d" % i, gs, lambda c, ps, i=i: k.act(
                    FB[:, 4 * i + c, :], ps, AF.Identity, bias=bglu[:, 4 * i + c:4 * i + c + 1]))
            for i in range(2):
                def ev_b(c, ps, i=i):
                    ci = 4 * i + c
                    tb = sq[:, ci % 4, :]
                    k.act(tb, ps, AF.Sigmoid, bias=bglu[:, 8 + ci:9 + ci])
                    k.tt("pool", FB[:, ci, :], FB[:, ci, :], tb, ALU.mult)
                yield from proj_fm_units("glu%d" % (2 + i), gs, ev_b)
            for i in range(2):
                def ev_zs(c, ps, i=i):
                    ci = 4 * i + c
                    tb = sq[:, 4 + ci % 4, :]
                    k.act(tb, ps, AF.Sigmoid)
                    k.tt("pool", FB[:, ci, :], FB[:, ci, :], tb, ALU.mult)
                    k.tt("dve", mixb[:, ci, :], FB[:, ci, :], FA[:, ci, :], ALU.add)
                yield from proj_fm_units("zs%d" % i, xn, ev_zs)

        units = None if meta else tail_units()

        if meta:
            return
        for _ in units:
            pass
        tap("mixb", AF_[:, 8192:12288], 4096)
        chk("b_gla", meta)
        tap("oT", AF_[:, 4096:8192], 4096)
        chk("b_og", meta)
        tap("og", rsb[:, :, :].rearrange("p a b -> p (a b)"), 4096)
        tap("mix", mixb[:, :, :].rearrange("p a b -> p (a b)"), 4096)
        for i in range(2):
            w = wnext("wout%d" % i)
            def ev_wo(c, ps, i=i):
                ci = 4 * i + c
                k.act(sq[:, ci, :], ps, AF.Square)
                k.copy("dve", FA[:, ci, :], ps)
            proj_fm(w, mixb, NTOK, ev_wo)
        norm_stats(FA[:, :, :], KT, TB, onesD, RS[:, 0, :], presq=True)
        for kt in range(KT):
            k.stt("dve", FA[:, kt, :], FA[:, kt, :], gpost[:, kt:kt + 1], RS[:, 0, :], ALU.mult, ALU.mult)
            k.tt("pool", hT[:, kt, :], hT[:, kt, :], FA[:, kt, :], ALU.add)
            k.act(sq[:, kt, :], hT[:, kt, :], AF.Square)
        chk("b_h1", meta)
        tap("h1", AF_[:, 0:4096], 4096)
        norm_stats(hT[:, :, :], KT, TB, onesD, RS[:, 1, :], presq=True)
        hn = Bv[0]
        for kt in range(KT):
            k.stt("dve", hn[:, kt, :], hT[:, kt, :], gfpre[:, kt:kt + 1], RS[:, 1, :], ALU.mult, ALU.mult)
        for i in range(8):
            w = wnext("ff1_%d" % i)

            def ev_f1(c, ps, i=i):
                ci = 4 * i + c
                tf = ATK[:, :] if ci % 2 else LNT[:, :]
                k.act(tf, ps, AF.Relu)
                k.tt("dve" if ci % 2 else "pool", hid[:, ci, :], tf, tf, ALU.mult)
            proj_fm(w, hn, NTOK, ev_f1)
        for half in range(2):
            b4 = [bank() for _ in range(4)]
            for rc in range(4):
                w3 = wnext("ff2_%d" % (4 * half + rc)).rearrange("p (a c) -> p a c", c=512)
                for c in range(4):
                    for kt in range(KT):
                        k.mm(b4[c][:, 0:512], w3[:, kt, 128 * c:128 * c + 128], hid[:, 8 * rc + kt, :],
                             start=(rc == 0 and kt == 0), stop=(rc == 3 and kt == KT - 1))
            for c in range(4):
                k.act(Bv[0][:, 4 * half + c, :], b4[c][:, 0:512], AF.Square)
                k.copy("dve", FA[:, 4 * half + c, :], b4[c][:, 0:512])
        norm_stats(FA[:, :, :], KT, TB, onesD, RS[:, 0, :], presq=True, sqb=Bv[0])
        for kt in range(KT):
            k.stt("dve", FA[:, kt, :], FA[:, kt, :], gfpost[:, kt:kt + 1], RS[:, 0, :], ALU.mult, ALU.mult)
            k.tt("pool", hT[:, kt, :], hT[:, kt, :], FA[:, kt, :], ALU.add)
        chk("b_ffn", meta)
        for t in range(ntile):
            xo = XOUT[:, t % 2, :]
            for half in range(2):
                b = bank()
                for q in range(4):
                    kt = 4 * half + q
                    k.tr(b[:, 128 * q:128 * q + 128], hT[:, kt, 128 * t:128 * t + 128], identF)
                evcopy(xo[:, 512 * half:512 * half + 512], b[:, 0:512])
            r0 = TB * bi + 128 * t
            k.dma("act", y_d[r0:r0 + 128, :], xo, "xout%d" % (t % 2))

    block(0, True)
    if stop_after == "meta":
        return early_exit()
    try:
        for bi in range(nblk):
            block(bi, False)
    except StopBuild:
        return early_exit()
    assert st["pos"] == len(seq)
    outs = [y_d] + list(dbg.values())
    regs = []
    for o in outs:
        regs.extend(o.regs.values())
    k.S.emit(regs)
    k.es.close()
    return nc, list(dbg.keys())


def _colT(v):
    return np.ascontiguousarray(np.asarray(v, np.float32).reshape(-1, 128).T)


def prep_common(inp):
    f = lambda a: np.ascontiguousarray(np.asarray(a, np.float32))
    pvec = np.concatenate([_colT(inp["g_mix_pre"][0]), _colT(inp["g_mix_post"][0]), _colT(inp["g_ffn_pre"][0]),
                           _colT(inp["g_ffn_post"][0]), _colT(np.asarray(inp["gla_norm_g"][0]).reshape(-1)),
                           _colT(inp["b_glu"][0]), _colT(inp["d_skip"][0])], axis=1)
    assert pvec.shape == (128, 64)
    return {
        "meta": f(inp["meta_tokens"]), "pvec": f(pvec), "w_in": f(inp["w_in"][0]), "w_o": f(inp["w_o_gla"][0]),
        "w_glu": f(inp["w_glu"][0]), "w_out": f(inp["w_out"][0]), "w_ff1": f(inp["w_ff1"][0]),
        "w_ff2": f(inp["w_ff2"][0]), "wgu": f(inp["w_gate_up"][0]), "bgate": f(np.asarray(inp["b_gate"][0])[None, :]),
        "are": f(inp["a_re"][0]), "aim": f(inp["a_im"][0]), "lstep": f(np.asarray(inp["log_step"][0])[:, None]),
        "bre": f(np.asarray(inp["b_re"][0]).reshape(64, 1024)), "bim": f(np.asarray(inp["b_im"][0]).reshape(64, 1024)),
        "cre": f(np.asarray(inp["c_re"][0]).reshape(64, 1024)), "cim": f(np.asarray(inp["c_im"][0]).reshape(64, 1024)),
    }


_CACHE = {}


def kernel(**inputs):
    x = np.asarray(inputs["x"], np.float32)
    bsz, seq, _ = x.shape
    nblk = seq // TB
    if nblk not in _CACHE:
        _CACHE[nblk] = build_program(nblk)[0]
    nc = _CACHE[nblk]
    common = prep_common(inputs)
    in_maps = []
    for b in range(bsz):
        m = dict(common)
        m["x"] = np.ascontiguousarray(x[b])
        in_maps.append(m)
    res = run_bass_kernel_spmd(nc, in_maps, core_ids=list(range(bsz)))
    return np.stack([np.asarray(r["y"], np.float32) for r in res.results], axis=0)
```
